# Optimizing a Trainium2 kernel written in Bass

```python
import math
import jax
import jax.numpy as jnp
from jax import lax
import numpy as np

D_MODEL = 1024
BATCH = 32
SEQ = 2048
DEPTH = 4

HYENA_CH = D_MODEL // 2
CONF_CH = D_MODEL // 2
HYENA_ORDER = 2
HYENA_SHORT_W = 3
HYENA_EMB_DIM = 33
HYENA_FILTER_DIM = 64
HYENA_SHORT_DECAY_PCT = 0.3
HYENA_LONG_DECAY_PCT = 1.5
HYENA_DECAY_TARGET = 1e-2
HYENA_FILTER_OUT_STD = 0.02
CONF_WIDTH = 31
EVEN_IN = 3 * HYENA_CH + 2 * CONF_CH
EVEN_MIX = HYENA_CH + CONF_CH
N_HEADS = 8
HEAD_DIM = 64
ATTN_W = N_HEADS * 2 * HEAD_DIM
Q_BLOCK = 128
N_EXPERTS = 32
TOP_K = 4
D_FF = D_MODEL
SWIGLU_LIMIT = 7.0
SWIGLU_ALPHA = 1.702
EXPERT_BLOCK = 512
DEEPNORM_ALPHA = (2 * DEPTH) ** 0.25
DEEPNORM_BETA = (8 * DEPTH) ** -0.25
N_EVEN = (DEPTH + 1) // 2
N_ODD = DEPTH // 2
LN_EPS = 1e-5

kernel_name = "hyena_conformer_diffattn_moe_deepnorm"


def layer_norm(x, g, b):
    xf = x.astype(jnp.float32)
    mu = jnp.mean(xf, axis=-1, keepdims=True)
    var = jnp.mean(jnp.square(xf - mu), axis=-1, keepdims=True)
    y = (xf - mu) * lax.rsqrt(var + LN_EPS) * g.astype(jnp.float32) + b.astype(jnp.float32)
    return y.astype(x.dtype)


def rms_norm(x, g):
    xf = x.astype(jnp.float32)
    y = xf * lax.rsqrt(jnp.mean(jnp.square(xf), axis=-1, keepdims=True) + LN_EPS)
    return (y * g.astype(jnp.float32)).astype(x.dtype)


def depthwise_conv(x, w, b):
    pad = w.shape[0] // 2
    y = lax.conv_general_dilated(x, w[:, None, :], window_strides=(1,),
                                 padding=[(pad, pad)],
                                 dimension_numbers=("NWC", "WIO", "NWC"),
                                 feature_group_count=x.shape[-1])
    return y + b


def alibi_slopes(n):
    return np.array([2.0 ** (-8.0 * (i + 1) / n) for i in range(n)], dtype=np.float32)


def hyena_filter_spectrum(L, f1_w, f1_b, f1_freq, f2_w, f2_b, f2_freq, f3_w):
    f32 = jnp.float32
    pos = jnp.arange(L, dtype=f32)
    t = jnp.linspace(0.0, 1.0, L, dtype=f32)[:, None]
    bands = (HYENA_EMB_DIM - 1) // 2
    f = jnp.linspace(1e-4, bands - 1, bands, dtype=f32)
    ang = (2.0 * math.pi / L) * pos[:, None] * f[None, :]
    feats = jnp.concatenate([t, jnp.cos(ang), -jnp.sin(ang)], axis=-1)
    h = jnp.sin(f1_freq.astype(f32) * (feats @ f1_w.astype(f32) + f1_b.astype(f32)))
    h = jnp.sin(f2_freq.astype(f32) * (h @ f2_w.astype(f32) + f2_b.astype(f32)))
    h = (h @ f3_w.astype(f32)).reshape(L, 2, HYENA_ORDER, HYENA_CH)
    max_decay = math.log(HYENA_DECAY_TARGET) / HYENA_SHORT_DECAY_PCT
    min_decay = math.log(HYENA_DECAY_TARGET) / HYENA_LONG_DECAY_PCT
    deltas = jnp.abs(jnp.linspace(min_decay, max_decay, HYENA_CH, dtype=f32))
    h = h * jnp.exp(-t * deltas[None, :])[:, None, None, :]
    fwd, bwd = h[:, 0], h[:, 1]
    two_sided = jnp.concatenate([fwd, jnp.zeros_like(fwd[:1]), bwd[:0:-1]], axis=0)
    return jnp.fft.rfft(two_sided, axis=0)


def long_conv(z, kf, skip):
    L = z.shape[1]
    zf32 = z.astype(jnp.float32)
    zf = jnp.fft.rfft(zf32, n=2 * L, axis=1)
    y = jnp.fft.irfft(zf * kf[None], n=2 * L, axis=1)[:, :L]
    return (y + zf32 * skip.astype(jnp.float32)).astype(z.dtype)


def hyena_conformer_mixer(x, w_in, b_in, short_w, short_b, f1_w, f1_b, f1_freq, f2_w, f2_b,
                          f2_freq, f3_w, skip, dw_w, dw_b, cln_g, cln_b, w_out, b_out):
    S = x.shape[1]
    proj = x @ w_in + b_in
    hy, cf = proj[..., :3 * HYENA_CH], proj[..., 3 * HYENA_CH:]
    hy = depthwise_conv(hy, short_w, short_b)
    x1, x2, v = jnp.split(hy, 3, axis=-1)
    kf = hyena_filter_spectrum(S, f1_w, f1_b, f1_freq, f2_w, f2_b, f2_freq, f3_w)
    z = x1 * long_conv(v, kf[:, 0], skip[0])
    z = x2 * long_conv(z, kf[:, 1], skip[1])
    a, g = jnp.split(cf, 2, axis=-1)
    u = a * jax.nn.sigmoid(g)
    u = depthwise_conv(u, dw_w, dw_b)
    u = jax.nn.silu(layer_norm(u, cln_g, cln_b))
    return jnp.concatenate([z, u], axis=-1) @ w_out + b_out


def diff_attention(x, w_qkv, lq1, lk1, lq2, lk2, subln_g, w_out, layer_idx):
    Bb, S, _ = x.shape
    f32 = jnp.float32
    q, k, v = jnp.split(x @ w_qkv, 3, axis=-1)
    q = q.reshape(Bb, S, N_HEADS, 2, HEAD_DIM) * (HEAD_DIM ** -0.5)
    k = k.reshape(Bb, S, N_HEADS, 2, HEAD_DIM)
    v = v.reshape(Bb, S, N_HEADS, 2 * HEAD_DIM)
    lam_init = 0.8 - 0.6 * math.exp(-0.3 * layer_idx)
    lam = (jnp.exp(jnp.sum(lq1.astype(f32) * lk1.astype(f32)))
           - jnp.exp(jnp.sum(lq2.astype(f32) * lk2.astype(f32))) + lam_init)
    slopes = jnp.asarray(alibi_slopes(N_HEADS))
    nq = S // Q_BLOCK
    qb = q.reshape(Bb, nq, Q_BLOCK, N_HEADS, 2, HEAD_DIM).transpose(1, 0, 2, 3, 4, 5)
    kpos = jnp.arange(S, dtype=jnp.int32)

    def block(args):
        qblk, i = args
        qpos = i * Q_BLOCK + jnp.arange(Q_BLOCK, dtype=jnp.int32)
        dist = jnp.abs(qpos[:, None] - kpos[None, :]).astype(f32)
        bias = -slopes[:, None, None] * dist[None]
        s = jnp.einsum("bqhcd,bkhcd->bhcqk", qblk, k).astype(f32) + bias[None, :, None]
        p = jax.nn.softmax(s, axis=-1)
        a = (p[:, :, 0] - lam * p[:, :, 1]).astype(v.dtype)
        return jnp.einsum("bhqk,bkhe->bqhe", a, v)

    o = lax.map(block, (qb, jnp.arange(nq, dtype=jnp.int32)))
    o = o.transpose(1, 0, 2, 3, 4).reshape(Bb, S, N_HEADS, 2 * HEAD_DIM)
    o = rms_norm(o, subln_g) * (1.0 - lam_init)
    return o.reshape(Bb, S, ATTN_W) @ w_out


def moe(x, w_r, b_r, w1, b1, w2, b2):
    Bb, S, D = x.shape
    T = Bb * S
    A = T * TOP_K
    xt = x.reshape(T, D)
    logits = (xt @ w_r).astype(jnp.float32) + b_r.astype(jnp.float32)
    top_val, top_idx = lax.top_k(logits, TOP_K)
    gate = jax.nn.softmax(top_val, axis=-1)
    e_flat = top_idx.reshape(A).astype(jnp.int32)
    tok_flat = jnp.arange(A, dtype=jnp.int32) // TOP_K
    order = jnp.argsort(e_flat)
    e_sorted = e_flat[order]
    counts = jnp.bincount(e_flat, length=N_EXPERTS).astype(jnp.int32)
    padded = (counts + EXPERT_BLOCK - 1) // EXPERT_BLOCK * EXPERT_BLOCK
    pad_end = jnp.cumsum(padded)
    pad_start = pad_end - padded
    grp_start = jnp.cumsum(counts) - counts
    dest = pad_start[e_sorted] + jnp.arange(A, dtype=jnp.int32) - grp_start[e_sorted]
    P = A + N_EXPERTS * EXPERT_BLOCK
    nb = P // EXPERT_BLOCK
    row_tok = jnp.zeros((P,), jnp.int32).at[dest].set(tok_flat[order])
    row_gate = jnp.zeros((P,), jnp.float32).at[dest].set(gate.reshape(A)[order])
    starts = jnp.arange(nb, dtype=jnp.int32) * EXPERT_BLOCK
    blk_expert = jnp.minimum(jnp.searchsorted(pad_end, starts, side="right"),
                             N_EXPERTS - 1).astype(jnp.int32)

    def expert_block(args):
        tok, g, e = args
        h = xt[tok] @ w1[e] + b1[e]
        hg, hu = h[:, :D_FF], h[:, D_FF:]
        hg = jnp.minimum(hg, SWIGLU_LIMIT)
        hu = jnp.clip(hu, -SWIGLU_LIMIT, SWIGLU_LIMIT)
        act = (hu + 1.0) * (hg * jax.nn.sigmoid(hg * SWIGLU_ALPHA))
        return (act @ w2[e] + b2[e]) * g[:, None]

    ys = lax.map(expert_block, (row_tok.reshape(nb, EXPERT_BLOCK),
                                row_gate.reshape(nb, EXPERT_BLOCK).astype(x.dtype),
                                blk_expert))
    out = jax.ops.segment_sum(ys.reshape(P, D), row_tok, num_segments=T)
    return out.reshape(Bb, S, D)


def setup_inputs(seed: int = 0) -> dict:
    key = jax.random.key(seed)
    keys = jax.random.split(key, 40)
    cnt = [0]

    def nrm(shape, scale=1.0):
        k = keys[cnt[0]]
        cnt[0] += 1
        return jax.random.normal(k, shape, jnp.float32) * scale

    beta = DEEPNORM_BETA
    inp = {}
    inp["x"] = nrm((BATCH, SEQ, D_MODEL))
    inp["hy_cf_w_in"] = nrm((N_EVEN, D_MODEL, EVEN_IN), D_MODEL ** -0.5)
    inp["hy_cf_b_in"] = nrm((N_EVEN, EVEN_IN), 0.02)
    inp["hy_short_w"] = nrm((N_EVEN, HYENA_SHORT_W, 3 * HYENA_CH), HYENA_SHORT_W ** -0.5)
    inp["hy_short_b"] = nrm((N_EVEN, 3 * HYENA_CH), 0.02)
    inp["hy_f1_w"] = nrm((N_EVEN, HYENA_EMB_DIM, HYENA_FILTER_DIM), HYENA_EMB_DIM ** -0.5)
    inp["hy_f1_b"] = nrm((N_EVEN, HYENA_FILTER_DIM), 0.02)
    inp["hy_f1_freq"] = 1.0 + nrm((N_EVEN, HYENA_FILTER_DIM), 0.1)
    inp["hy_f2_w"] = nrm((N_EVEN, HYENA_FILTER_DIM, HYENA_FILTER_DIM), HYENA_FILTER_DIM ** -0.5)
    inp["hy_f2_b"] = nrm((N_EVEN, HYENA_FILTER_DIM), 0.02)
    inp["hy_f2_freq"] = 1.0 + nrm((N_EVEN, HYENA_FILTER_DIM), 0.1)
    inp["hy_f3_w"] = nrm((N_EVEN, HYENA_FILTER_DIM, 2 * HYENA_ORDER * HYENA_CH), HYENA_FILTER_OUT_STD)
    inp["hy_skip"] = nrm((N_EVEN, HYENA_ORDER, HYENA_CH))
    inp["cf_dw_w"] = nrm((N_EVEN, CONF_WIDTH, CONF_CH), CONF_WIDTH ** -0.5)
    inp["cf_dw_b"] = nrm((N_EVEN, CONF_CH), 0.02)
    inp["cf_ln_g"] = 1.0 + nrm((N_EVEN, CONF_CH), 0.02)
    inp["cf_ln_b"] = nrm((N_EVEN, CONF_CH), 0.02)
    inp["even_w_out"] = nrm((N_EVEN, EVEN_MIX, D_MODEL), EVEN_MIX ** -0.5 * beta)
    inp["even_b_out"] = nrm((N_EVEN, D_MODEL), 0.02)
    v_scale = jnp.concatenate([jnp.ones((2 * ATTN_W,), jnp.float32),
                               jnp.full((ATTN_W,), beta, jnp.float32)])
    inp["attn_w_qkv"] = nrm((N_ODD, D_MODEL, 3 * ATTN_W), D_MODEL ** -0.5) * v_scale
    inp["attn_lq1"] = nrm((N_ODD, HEAD_DIM), 0.1)
    inp["attn_lk1"] = nrm((N_ODD, HEAD_DIM), 0.1)
    inp["attn_lq2"] = nrm((N_ODD, HEAD_DIM), 0.1)
    inp["attn_lk2"] = nrm((N_ODD, HEAD_DIM), 0.1)
    inp["attn_subln_g"] = 1.0 + nrm((N_ODD, 2 * HEAD_DIM), 0.02)
    inp["attn_w_out"] = nrm((N_ODD, ATTN_W, D_MODEL), ATTN_W ** -0.5 * beta)
    inp["ln1_g"] = 1.0 + nrm((DEPTH, D_MODEL), 0.02)
    inp["ln1_b"] = nrm((DEPTH, D_MODEL), 0.02)
    inp["ln2_g"] = 1.0 + nrm((DEPTH, D_MODEL), 0.02)
    inp["ln2_b"] = nrm((DEPTH, D_MODEL), 0.02)
    inp["moe_w_r"] = nrm((DEPTH, D_MODEL, N_EXPERTS), D_MODEL ** -0.5)
    inp["moe_b_r"] = nrm((DEPTH, N_EXPERTS), 0.01)
    inp["moe_w1"] = nrm((DEPTH, N_EXPERTS, D_MODEL, 2 * D_FF), D_MODEL ** -0.5 * beta)
    inp["moe_b1"] = nrm((DEPTH, N_EXPERTS, 2 * D_FF), 0.02)
    inp["moe_w2"] = nrm((DEPTH, N_EXPERTS, D_FF, D_MODEL), D_FF ** -0.5 * beta)
    inp["moe_b2"] = nrm((DEPTH, N_EXPERTS, D_MODEL), 0.02)
    return inp


def reference(x, hy_cf_w_in, hy_cf_b_in, hy_short_w, hy_short_b, hy_f1_w, hy_f1_b, hy_f1_freq,
              hy_f2_w, hy_f2_b, hy_f2_freq, hy_f3_w, hy_skip, cf_dw_w, cf_dw_b, cf_ln_g, cf_ln_b,
              even_w_out, even_b_out, attn_w_qkv, attn_lq1, attn_lk1, attn_lq2, attn_lk2,
              attn_subln_g, attn_w_out, ln1_g, ln1_b, ln2_g, ln2_b, moe_w_r, moe_b_r,
              moe_w1, moe_b1, moe_w2, moe_b2):
    for i in range(DEPTH):
        j = i // 2
        if i % 2 == 0:
            m = hyena_conformer_mixer(x, hy_cf_w_in[j], hy_cf_b_in[j], hy_short_w[j], hy_short_b[j],
                                      hy_f1_w[j], hy_f1_b[j], hy_f1_freq[j], hy_f2_w[j], hy_f2_b[j],
                                      hy_f2_freq[j], hy_f3_w[j], hy_skip[j], cf_dw_w[j], cf_dw_b[j],
                                      cf_ln_g[j], cf_ln_b[j], even_w_out[j], even_b_out[j])
        else:
            m = diff_attention(x, attn_w_qkv[j], attn_lq1[j], attn_lk1[j], attn_lq2[j], attn_lk2[j],
                               attn_subln_g[j], attn_w_out[j], i)
        x = layer_norm(DEEPNORM_ALPHA * x + m, ln1_g[i], ln1_b[i])
        f = moe(x, moe_w_r[i], moe_b_r[i], moe_w1[i], moe_b1[i], moe_w2[i], moe_b2[i])
        x = layer_norm(DEEPNORM_ALPHA * x + f, ln2_g[i], ln2_b[i])
    return x
```

```python
import math
from contextlib import ExitStack

import numpy as np
import ml_dtypes

import concourse.bass as bass
import concourse.mybir as mybir
from concourse.bass_utils import run_bass_kernel_spmd

F32 = mybir.dt.float32
BF16 = mybir.dt.bfloat16
AF = mybir.ActivationFunctionType
ALU = mybir.AluOpType

D = 1024
S = 2048
NT = S // 128
DEPTH = 4
ALPHA = (2 * DEPTH) ** 0.25
LN_EPS = 1e-5


class Res:
    __slots__ = ("name", "w", "r")
    ALL = []

    def __init__(self, name):
        self.name = name
        self.w = []
        self.r = []
        Res.ALL.append(self)


class Prog:
    ENG = ("pe", "act", "dve", "pool", "sp")
    NDMA = 12

    def __init__(self, nc, stack):
        self.nc = nc
        self.ops = {e: [] for e in self.ENG}
        self.cnt = {e: 0 for e in self.ENG}
        self.known = {e: {} for e in self.ENG}
        self.sem = {}
        for e in ("pe", "act", "dve", "pool"):
            self.sem[e] = stack.enter_context(nc.semaphore("s_" + e))
        for k in ("arr", "go", "ack"):
            self.sem[k] = stack.enter_context(nc.semaphore("b_" + k))
        Res.ALL = []
        self.dma_n = {}
        for q in ("sp", "pool", "act"):
            self.dma_n[q] = 0
            for i in range(self.NDMA):
                self.sem[("dma", q, i)] = stack.enter_context(nc.semaphore("d_%s_%d" % (q, i)))

    def _waits(self, eng, reads, writes):
        toks = []
        for r in reads:
            toks.extend(r.w)
        for r in writes:
            toks.extend(r.w)
            toks.extend(r.r)
        out = []
        kn = self.known[eng]
        best = {}
        for (k, v) in toks:
            if eng == "pe" and k == "pe":
                continue
            if v > kn.get(k, 0) and v > best.get(k, 0):
                best[k] = v
        for k, v in best.items():
            kn[k] = v
            out.append((k, v))
        return out

    def op(self, eng, fn, reads=(), writes=()):
        waits = self._waits(eng, reads, writes)
        self.cnt[eng] += 1
        tok = (eng, self.cnt[eng])
        self.ops[eng].append((waits, fn, (eng, 1)))
        for r in reads:
            r.r.append(tok)
        for r in writes:
            r.w = [tok]
            r.r = []
        return tok

    def o(self, eng, meth, reads=(), writes=(), **kw):
        return self.op(eng, lambda e, m=meth, kw=kw: getattr(e, m)(**kw), reads=reads, writes=writes)

    def dma(self, q, out, in_, reads=(), writes=()):
        i = self.dma_n[q]
        self.dma_n[q] += 1
        slot = i % self.NDMA
        key = ("dma", q, slot)
        prev = 16 * (i // self.NDMA)
        waits = self._waits(q, reads, writes)
        kn = self.known[q]
        if prev > kn.get(key, 0):
            kn[key] = prev
            waits.append((key, prev))
        tok = (key, prev + 16)
        self.ops[q].append((waits, lambda e, lv=None, o=out, i_=in_: e.dma_start(out=(o(lv) if callable(o) else o), in_=(i_(lv) if callable(i_) else i_)), (key, 16), "dma"))
        for r in reads:
            r.r.append(tok)
        for r in writes:
            r.w = [tok]
            r.r = []
        return tok

    def wait_all(self, eng, toks):
        waits = []
        kn = self.known[eng]
        for (k, v) in toks:
            if v > kn.get(k, 0):
                kn[k] = v
                waits.append((k, v))
        self.ops[eng].append((waits, None, None))

    def barrier(self, res_list):
        toks = [(e, self.cnt[e]) for e in ("pe", "act", "dve", "pool") if self.cnt[e] > 0]
        for q in ("sp", "pool", "act"):
            n = self.dma_n[q]
            for slot in range(min(n, self.NDMA)):
                last = ((n - 1 - slot) // self.NDMA) * self.NDMA + slot
                toks.append((("dma", q, slot), 16 * (last // self.NDMA + 1)))
        for r in res_list:
            r.r = list(toks)

    def loop_begin(self, n):
        for e in self.ENG:
            self.ops[e].append(("LB", n))

    def loop_end(self):
        for e in self.ENG:
            self.ops[e].append(("LE",))

    def sync_reset(self):
        def dma_fin(q):
            n = self.dma_n[q]
            return [(("dma", q, s_), 16 * ((n - 1 - s_) // self.NDMA + 1)) for s_ in range(min(n, self.NDMA))]
        for e in ("pe", "act", "dve", "pool"):
            w = [(e, self.cnt[e])] if self.cnt[e] > 0 else []
            if e in ("act", "pool"):
                w += dma_fin(e)
            self.ops[e].append(("BAR", w))
        self.ops["sp"].append(("BARSP", dma_fin("sp")))
        for e in self.ENG:
            self.cnt[e] = 0
            self.known[e] = {}
        for q in self.dma_n:
            self.dma_n[q] = 0
        for r in Res.ALL:
            r.w = []
            r.r = []

    def emit(self):
        nc = self.nc
        names = {"pe": "tensor", "act": "scalar", "dve": "vector", "pool": "gpsimd", "sp": "sync"}
        sem = self.sem
        body_sems = [v for k, v in sem.items() if k not in ("go", "ack")]

        def run(e, ops, pos, lv):
            while pos < len(ops):
                op_ = ops[pos]
                if op_[0] == "LB":
                    with e.Fori(0, op_[1]) as i:
                        pos = run(e, ops, pos + 1, i)
                    continue
                if op_[0] == "LE":
                    return pos + 1
                if op_[0] == "BAR":
                    for (k, v) in op_[1]:
                        e.wait_ge(sem[k], v)
                    e.sem_inc(sem["arr"], 1)
                    e.wait_ge(sem["go"], 1)
                    e.sem_inc(sem["ack"], 1)
                    pos += 1
                    continue
                if op_[0] == "BARSP":
                    for (k, v) in op_[1]:
                        e.wait_ge(sem[k], v)
                    e.wait_ge(sem["arr"], 4)
                    for sh in body_sems:
                        e.sem_clear(sh)
                    e.sem_inc(sem["go"], 1)
                    e.wait_ge(sem["ack"], 4)
                    e.sem_clear(sem["go"])
                    e.sem_clear(sem["ack"])
                    pos += 1
                    continue
                waits, fn, inc = op_[0], op_[1], op_[2]
                for (k, v) in waits:
                    e.wait_ge(sem[k], v)
                if fn is not None:
                    ins = fn(e, lv) if len(op_) == 4 else fn(e)
                    ins.then_inc(sem[inc[0]], inc[1])
                pos += 1
            return pos

        with nc.Block() as block:
            for eng in self.ENG:
                ops = self.ops[eng]
                getattr(block, names[eng])(lambda e, ops=ops: run(e, ops, 0, None))


class Ctx:
    pass


def build_program(cfg):
    NSEQ = cfg.get("nseq", 4)
    layers = cfg.get("layers", list(range(DEPTH)))
    nc = bass.Bass("TRN2", target_bir_lowering=False)
    with ExitStack() as stack:
        P = Prog(nc, stack)
        c = Ctx()
        c.nc, c.P, c.cfg, c.stack = nc, P, cfg, stack

        def dram_in(name, shape, dt=F32):
            return nc.dram_tensor(name, list(shape), dt, kind="ExternalInput").ap()

        def sb(name, shape, dt=F32):
            return stack.enter_context(nc.sbuf_tensor(name, list(shape), dt))

        c.dram_in, c.sb = dram_in, sb
        x_d = dram_in("x", [NSEQ, S, D])
        y_d = nc.dram_tensor("y", [NSEQ, S, D], F32, kind="ExternalOutput").ap()
        ident_d = dram_in("ident", [128, 128])
        lnp_d = dram_in("lnp", [DEPTH, 4, 128, D])

        NE = cfg.get("ne", 32)
        c.NE = NE
        c.wr_d = dram_in("moe_wr", [DEPTH, 128, 8, NE])
        c.br_d = dram_in("moe_br", [DEPTH, 128, NE])
        c.w1_d = dram_in("moe_w1", [DEPTH * NE * 8, 128, 8 * 256])
        c.w2_d = dram_in("moe_w2", [DEPTH * NE, 128, 8 * 1024])
        c.b1_d = dram_in("moe_b1", [DEPTH, 128, NE * 16])
        c.b2_d = dram_in("moe_b2", [DEPTH, NE, D])

        c.attn_d = {
            "LB": dram_in("att_LB", [128, 8, 2, 128], BF16), "RB": dram_in("att_RB", [128, 2, 512], BF16),
            "DG": dram_in("att_DG", [128, 8, 128], BF16), "idb": dram_in("att_idb", [128, 128], BF16),
            "cbt": dram_in("att_cbt", [128, 8, 31]), "wh": dram_in("att_wh", [16, 128, 8 * 384]),
            "wo": dram_in("att_wo", [2, 128, 8 * D]), "lqk": dram_in("att_lqk", [2, 128, 4, 64]),
            "subg": dram_in("att_subg", [2, 128, 128])}
        c.hy_d = {
            "featsT": dram_in("hy_featsT", [33, S]), "decay": dram_in("hy_decay", [128, 16, 512]),
            "F": dram_in("hy_F", [16, 128, 16 * 2 * 128], BF16), "G": dram_in("hy_G", [16, 128, 16 * 2 * 128], BF16),
            "sgn": dram_in("hy_sgn", [128, 16]), "f1w": dram_in("hy_f1w", [2, 33, 64]), "f2w": dram_in("hy_f2w", [2, 64, 64]),
            "f3w": dram_in("hy_f3w", [2, 64, 2048]), "fvec": dram_in("hy_fvec", [2, 64, 8]), "skip": dram_in("hy_skip", [2, 128, 2, 512]),
            "KF": nc.dram_tensor("hy_KF", [2 * 2 * 16, 128, 2, 512], F32, kind="Internal").ap()}
        c.r_KF = [[[Res("KF") for fi in range(16)] for o in range(2)] for j in range(2)]
        c.ev_d = {
            "win": dram_in("ev_win", [40, 128, 8 * 128]), "vec": dram_in("ev_vec", [2, 128, EV_N]),
            "wo": dram_in("ev_wo", [2, 128, 8 * D]), "bo": dram_in("ev_bo", [2, 1, D]), "ones": dram_in("ones128", [128, 128]),
            "X12": nc.dram_tensor("ev_X12", [2, 4, 128, 16 * 128], BF16, kind="Internal").ap()}
        c.r_X12 = [[Res("X12") for cc in range(4)] for w_ in range(2)]
        c.xres = sb("xres", [128, NT, D])
        c.xT = sb("xT", [128, 8, S], BF16)
        c.ident = sb("ident_sb", [128, 128])
        c.G = sb("G", [128, NT, NE])
        c.stats = [sb("stats%d" % i, [128, 2, 6]) for i in range(2)]
        c.mv = [sb("mv%d" % i, [128, 8]) for i in range(2)]
        c.epsc = sb("epsc", [128, 1])
        c.ps = [stack.enter_context(nc.psum_tensor("ps%d" % i, [128, 512], F32)) for i in range(8)]
        OVB = 109 * 1024
        c.ov = sb("ov", [128, OVB // 4])
        c.ov_off = 0
        c.bar_toks = []

        def carve(shape, dt=F32):
            n = 1
            for d_ in shape[1:]:
                n *= d_
            nb = n * (4 if dt == F32 else 2)
            nb = (nb + 31) // 32 * 32
            assert c.ov_off + nb <= OVB, ("overlay overflow", c.ov_off, nb)
            v = c.ov[0:shape[0], c.ov_off // 4:(c.ov_off + nb) // 4]
            c.ov_off += nb
            if dt != F32:
                v = v.bitcast(dt)
            v = v[:, 0:n]
            if len(shape) == 3:
                v = v.rearrange("p (a b) -> p a b", b=shape[2])
            elif len(shape) == 4:
                v = v.rearrange("p (a b c) -> p a b c", b=shape[2], c=shape[3])
            return v

        def newres(name):
            r = Res(name)
            r.r = list(c.bar_toks)
            r.w = list(c.bar_toks)
            return r

        def phase(name):
            c.ov_off = 0
            toks = [(e, P.cnt[e]) for e in ("pe", "act", "dve", "pool") if P.cnt[e] > 0]
            for q in ("sp", "pool", "act"):
                n = P.dma_n[q]
                for slot in range(min(n, P.NDMA)):
                    last = ((n - 1 - slot) // P.NDMA) * P.NDMA + slot
                    toks.append((("dma", q, slot), 16 * (last // P.NDMA + 1)))
            c.bar_toks = toks

        def touch_all(rl):
            phase("touch")
            for r_ in rl:
                r_.r = list(c.bar_toks)
                r_.w = list(c.bar_toks)

        c.touch_all = touch_all
        c.carve, c.newres, c.phase = carve, newres, phase
        c.dbg_toks = []

        def dbg(name, ap, shape, dt, reads):
            if not cfg.get("debug"):
                return
            t_ = nc.dram_tensor("dbg_" + name, list(shape), dt, kind="ExternalOutput").ap()
            c.dbg_toks.append(P.dma("sp", t_, ap, reads=reads))

        c.dbg = dbg

        c.r_xres = [Res("xres%d" % t) for t in range(NT)]
        c.r_xT = [(Res("xTa%d" % t), Res("xTb%d" % t)) for t in range(NT)]
        c.r_ps = [Res("ps%d" % i) for i in range(8)]
        c.r_ident = Res("ident")
        c.r_stats = [Res("st0"), Res("st1")]
        c.r_mv = [Res("mv0"), Res("mv1")]
        c.r_eps = Res("eps")
        c.r_G = [Res("G%d" % t) for t in range(NT)]
        c.r_GT = [Res("GT%d" % t) for t in range(NT)]
        c.lnp_d = lnp_d

        P.dma("sp", c.ident[:], ident_d[:, :], writes=[c.r_ident])
        P.op("dve", lambda e: e.memset(c.epsc[:], LN_EPS), writes=[c.r_eps])

        out_toks = []
        if cfg.get("mixer", True):
            hyena_prologue(c)
        if cfg.get("debug_kf"):
            kf_o = nc.dram_tensor("dbg_KF", [64, 128, 2, 512], F32, kind="ExternalOutput").ap()
            for i_ in range(32):
                c.dbg_toks.append(P.dma("sp", kf_o[i_], c.hy_d["KF"][i_], reads=[c.r_KF[0][i_ // 16][i_ % 16]]))
        if cfg.get("only_prologue"):
            layers = []
        P.sync_reset()
        P.loop_begin(NSEQ)
        c.bar_toks = []
        for t in range(NT):
            P.dma("sp", c.xres[:, t, :], (lambda lv, t=t: x_d[lv, t * 128:(t + 1) * 128, :]), writes=[c.r_xres[t]])
        build_xT(c)
        for l in layers:
            if cfg.get("mixer", True):
                if l % 2 == 0:
                    even_mixer(c, l // 2)
                else:
                    attention(c, l // 2)
            else:
                for t in range(NT):
                    P.op("pool", lambda e, t=t: e.tensor_scalar(out=c.xres[:, t, :], in0=c.xres[:, t, :], scalar1=ALPHA, scalar2=None, op0=ALU.mult), reads=[c.r_xres[t]], writes=[c.r_xres[t]])
            layer_norm(c, l, 0, router=True)
            if cfg.get("do_moe", True):
                moe(c, l)
                layer_norm(c, l, 1)
        for t in range(NT):
            P.dma("sp", (lambda lv, t=t: y_d[lv, t * 128:(t + 1) * 128, :]), c.xres[:, t, :], reads=[c.r_xres[t]])
        P.sync_reset()
        P.loop_end()
        P.emit()
    return nc


def make_stager(c, width, n=2):
    P = c.P
    st = [c.carve([128, width]) for i in range(n)]
    rs = [c.newres("stg") for i in range(n)]
    k = [0]

    def ld(dst, src, wres, view=None, eng="pool"):
        i = k[0] % n; k[0] += 1
        free = 1
        for d_ in dst.shape[1:]:
            free *= d_
        sv = st[i][:, 0:free]
        if view is not None:
            sv = view(sv)
        P.dma("sp", sv, src, writes=[rs[i]])
        if eng == "act":
            P.o("act", "activation", reads=[rs[i]], writes=wres, out=dst, in_=sv, func=AF.Copy)
        else:
            P.o(eng, "tensor_copy", reads=[rs[i]], writes=wres, out=dst, in_=sv)

    return ld


def build_xT(c):
    P = c.P
    c.phase("xT0")
    n = 0
    for t in range(NT):
        for half in range(2):
            pb = n % 4; n += 1
            ps, rps = c.ps[pb], c.r_ps[pb]
            for j in range(4):
                kc = half * 4 + j
                P.o("pe", "transpose", reads=[c.r_xres[t], c.r_ident], writes=[rps], out=ps[:, j * 128:(j + 1) * 128], in_=c.xres[:, t, kc * 128:(kc + 1) * 128], identity=c.ident[:])
            dst = c.xT[:, half * 4:(half + 1) * 4, t * 128:(t + 1) * 128]
            src = ps[:].rearrange("p (a b) -> p a b", b=128)
            if half == 0:
                P.o("act", "activation", reads=[rps], writes=[c.r_xT[t][half]], out=dst, in_=src, func=AF.Copy)
            else:
                P.o("dve", "tensor_copy", reads=[rps], writes=[c.r_xT[t][half]], out=dst, in_=src)


def layer_norm(c, l, which, router=False):
    P = c.P
    NE = c.NE
    c.phase("ln")
    c.GT = c.carve([32, S])
    c.r_GT = [c.newres("GT") for t in range(NT)]
    c.lng, c.lnb = c.carve([128, D]), c.carve([128, D])
    c.r_lng, c.r_lnb = c.newres("lng"), c.newres("lnb")
    c.lntmp = [c.carve([128, D]) for i in range(2)]
    c.r_lntmp = [c.newres("lntmp0"), c.newres("lntmp1")]
    c.xTf = [c.carve([128, 4, 128]) for i in range(2)]
    c.r_xTf = [c.newres("xTf0"), c.newres("xTf1")]
    c.wr, c.r_wr = c.carve([128, 8, NE]), c.newres("wr")
    c.br, c.r_br = c.carve([128, NE]), c.newres("br")
    c.rt = [c.carve([128, 4 * NE + 16]) for i in range(2)]
    c.r_rt = [c.newres("rt0"), c.newres("rt1")]
    P.dma("sp", c.lng, c.lnp_d[l, 2 * which, :, :], writes=[c.r_lng])
    P.dma("sp", c.lnb, c.lnp_d[l, 2 * which + 1, :, :], writes=[c.r_lnb])
    if router:
        P.dma("sp", c.wr, c.wr_d[l], writes=[c.r_wr])
        P.dma("sp", c.br, c.br_d[l], writes=[c.r_br])
    for t in range(NT):
        b = t % 2
        xr = c.xres[:, t, :]
        rx = c.r_xres[t]
        st, mv, tmp = c.stats[b], c.mv[b], c.lntmp[b]
        rst, rmv, rtmp = c.r_stats[b], c.r_mv[b], c.r_lntmp[b]
        P.op("dve", lambda e, st=st, xr=xr: e.bn_stats(out=st[:, 0, :], in_=xr[:, 0:512]), reads=[rx], writes=[rst])
        P.op("dve", lambda e, st=st, xr=xr: e.bn_stats(out=st[:, 1, :], in_=xr[:, 512:1024]), reads=[rx], writes=[rst])
        P.op("dve", lambda e, st=st, mv=mv: e.bn_aggr(out=mv[:, 0:2], in_=st[:].rearrange("p a b -> p (a b)")), reads=[rst], writes=[rmv])
        P.op("act", lambda e, mv=mv: e.activation(out=mv[:, 2:3], in_=mv[:, 1:2], func=AF.Sqrt, bias=c.epsc[:], scale=1.0), reads=[rmv, c.r_eps], writes=[rmv])
        P.op("dve", lambda e, mv=mv: e.reciprocal(out=mv[:, 3:4], in_=mv[:, 2:3]), reads=[rmv], writes=[rmv])
        P.op("dve", lambda e, mv=mv: e.tensor_scalar(out=mv[:, 4:5], in0=mv[:, 0:1], scalar1=mv[:, 3:4], scalar2=-1.0, op0=ALU.mult, op1=ALU.mult), reads=[rmv], writes=[rmv])
        P.op("act", lambda e, mv=mv, tmp=tmp, xr=xr: e.activation(out=tmp[:], in_=xr, func=AF.Identity, bias=mv[:, 4:5], scale=mv[:, 3:4]), reads=[rx, rmv], writes=[rtmp])
        P.op("pool", lambda e, tmp=tmp: e.tensor_tensor(out=tmp[:], in0=tmp[:], in1=c.lng[:], op=ALU.mult), reads=[rtmp, c.r_lng], writes=[rtmp])
        P.op("dve", lambda e, tmp=tmp, xr=xr: e.tensor_tensor(out=xr, in0=tmp[:], in1=c.lnb[:], op=ALU.add), reads=[rtmp, c.r_lnb], writes=[rx])
        for half in range(2):
            pb = 2 * b + half
            ps, rps = c.ps[pb], c.r_ps[pb]
            xf, rxf = c.xTf[half], c.r_xTf[half]
            for j in range(4):
                kc = half * 4 + j
                P.op("pe", lambda e, ps=ps, xr=xr, kc=kc, j=j: e.transpose(out=ps[:, j * 128:(j + 1) * 128], in_=xr[:, kc * 128:(kc + 1) * 128], identity=c.ident[:]), reads=[rx, c.r_ident], writes=[rps])
            dst = c.xT[:, half * 4:(half + 1) * 4, t * 128:(t + 1) * 128]
            if half == 0:
                P.op("act", lambda e, ps=ps, xf=xf: e.activation(out=xf[:], in_=ps[:].rearrange("p (a b) -> p a b", b=128), func=AF.Copy), reads=[rps], writes=[rxf])
            else:
                P.op("dve", lambda e, ps=ps, xf=xf: e.tensor_copy(out=xf[:], in_=ps[:].rearrange("p (a b) -> p a b", b=128)), reads=[rps], writes=[rxf])
            P.op("pool", lambda e, xf=xf, dst=dst: e.tensor_copy(out=dst, in_=xf[:]), reads=[rxf], writes=[c.r_xT[t][half]])
        if router:
            NE = c.NE
            psl, rpsl = c.ps[4 + b], c.r_ps[4 + b]
            for kc in range(8):
                xf = c.xTf[kc // 4]
                P.op("pe", lambda e, psl=psl, xf=xf, kc=kc: e.matmul(psl[:, 0:NE], lhsT=xf[:, kc % 4, :], rhs=c.wr[:, kc, :], start=(kc == 0), stop=(kc == 7)), reads=[c.r_xTf[kc // 4], c.r_wr], writes=[rpsl])
            rt, rrt = c.rt[b], c.r_rt[b]
            lg, ex, mk, t8 = rt[:, 0:NE], rt[:, NE:2 * NE], rt[:, 2 * NE:3 * NE], rt[:, 4 * NE:4 * NE + 8]
            nm, ss = rt[:, 4 * NE + 8:4 * NE + 9], rt[:, 4 * NE + 9:4 * NE + 10]
            Gt = c.G[:, t, :]
            P.op("dve", lambda e, lg=lg, psl=psl: e.tensor_tensor(out=lg, in0=psl[:, 0:NE], in1=c.br[:], op=ALU.add), reads=[rpsl, c.r_br], writes=[rrt])
            P.op("dve", lambda e, lg=lg, t8=t8: e.max(out=t8, in_=lg), reads=[rrt], writes=[rrt])
            P.op("dve", lambda e, lg=lg, t8=t8, mk=mk: e.tensor_scalar(out=mk, in0=lg, scalar1=t8[:, 3:4], scalar2=None, op0=ALU.is_ge), reads=[rrt], writes=[rrt])
            P.op("dve", lambda e, nm=nm, t8=t8: e.tensor_scalar(out=nm, in0=t8[:, 0:1], scalar1=-1.0, scalar2=None, op0=ALU.mult), reads=[rrt], writes=[rrt])
            P.op("act", lambda e, ex=ex, lg=lg, nm=nm: e.activation(out=ex, in_=lg, func=AF.Exp, bias=nm, scale=1.0), reads=[rrt], writes=[rrt])
            P.op("dve", lambda e, ex=ex, mk=mk: e.tensor_tensor(out=ex, in0=ex, in1=mk, op=ALU.mult), reads=[rrt], writes=[rrt])
            P.op("dve", lambda e, ex=ex, ss=ss: e.reduce_sum(out=ss, in_=ex, axis=mybir.AxisListType.X), reads=[rrt], writes=[rrt])
            P.op("dve", lambda e, ss=ss: e.reciprocal(out=ss, in_=ss), reads=[rrt], writes=[rrt])
            P.op("dve", lambda e, ex=ex, ss=ss, Gt=Gt: e.tensor_scalar(out=Gt, in0=ex, scalar1=ss, scalar2=None, op0=ALU.mult), reads=[rrt], writes=[c.r_G[t]])
            psg, rpsg = c.ps[6 + b], c.r_ps[6 + b]
            P.op("pe", lambda e, psg=psg, Gt=Gt: e.transpose(out=psg[0:NE, 0:128], in_=Gt, identity=c.ident[:]), reads=[c.r_G[t], c.r_ident], writes=[rpsg])
            P.op("act", lambda e, psg=psg, t=t: e.activation(out=c.GT[0:NE, t * 128:(t + 1) * 128], in_=psg[0:NE, 0:128], func=AF.Copy), reads=[rpsg], writes=[c.r_GT[t]])


def moe(c, l):
    P, NE = c.P, c.NE
    c.phase("moe")
    c.GT = c.carve([32, S])
    c.r_GT = [c.newres("GT") for t in range(NT)]
    c.actT = c.carve([128, 8, S], BF16); c.r_actT = [[c.newres("actT%d_%d" % (j, t)) for t in range(4)] for j in range(8)]
    c.w2 = [c.carve([128, 8, 1024], BF16) for i in range(1)]; c.r_w2 = [[c.newres("w2_%d_%d" % (i, h)) for h in range(8)] for i in range(1)]
    ld_w1 = make_stager(c, 2048, 2)
    ld_w2 = make_stager(c, 1024, 2)
    c.W1R = 2
    c.w1 = [c.carve([128, 8, 256], BF16) for i in range(c.W1R)]; c.r_w1 = [c.newres("w1_%d" % i) for i in range(c.W1R)]
    c.tA = [c.carve([128, 512]) for i in range(2)]; c.r_tA = [c.newres("tA0"), c.newres("tA1")]
    c.tB = [c.carve([128, 512]) for i in range(2)]; c.r_tB = [c.newres("tB0"), c.newres("tB1")]
    c.tC = [c.carve([128, 512]) for i in range(2)]; c.r_tC = [c.newres("tC0"), c.newres("tC1")]
    c.b1, c.r_b1 = c.carve([128, NE * 16]), c.newres("b1")
    c.b2, c.r_b2 = c.carve([32, D]), c.newres("b2")
    c.w1n = 0
    P.dma("sp", c.b1[:], c.b1_d[l], writes=[c.r_b1])
    P.dma("sp", c.b2[0:NE, :], c.b2_d[l], writes=[c.r_b2])
    n = 0
    for ts in range(NT):
        for dh in range(2):
            pb = 4 + (n % 4); n += 1
            ps, rps = c.ps[pb], c.r_ps[pb]
            P.op("pe", lambda e, ps=ps, ts=ts, dh=dh: e.matmul(ps[:], lhsT=c.GT[0:NE, ts * 128:(ts + 1) * 128], rhs=c.b2[0:NE, dh * 512:(dh + 1) * 512], start=True, stop=True), reads=[c.r_GT[ts], c.r_b2], writes=[rps])
            xr = c.xres[:, ts, dh * 512:(dh + 1) * 512]
            P.op("dve", lambda e, ps=ps, xr=xr: e.scalar_tensor_tensor(out=xr, in0=xr, scalar=ALPHA, in1=ps[:], op0=ALU.mult, op1=ALU.add), reads=[rps, c.r_xres[ts]], writes=[c.r_xres[ts]])
    hn = 0
    for ex in range(NE):
        w2, rw2 = c.w2[0], c.r_w2[0]
        for j in range(8):
            wi = c.w1n % c.W1R; c.w1n += 1
            w1, rw1 = c.w1[wi], c.r_w1[wi]
            ld_w1(w1, c.w1_d[(l * NE + ex) * 8 + j].rearrange("p (a b) -> p a b", b=256), [rw1], view=lambda v: v.rearrange("p (a b) -> p a b", b=256), eng="pool")
            ld_w2(w2[:, j, :], c.w2_d[l * NE + ex][:, j * 1024:(j + 1) * 1024], [rw2[j]], eng="act")
            for tt in range(4):
                sl = hn % 2; hn += 1
                pg, pu = c.ps[2 * sl], c.ps[2 * sl + 1]
                rpg, rpu = c.r_ps[2 * sl], c.r_ps[2 * sl + 1]
                xrd = [r for t in range(tt * 4, tt * 4 + 4) for r in c.r_xT[t]]
                for kc in range(8):
                    P.op("pe", lambda e, pg=pg, w1=w1, kc=kc, tt=tt: e.matmul(pg[:], lhsT=w1[:, kc, 0:128], rhs=c.xT[:, kc, tt * 512:(tt + 1) * 512], start=(kc == 0), stop=(kc == 7)), reads=[rw1] + xrd, writes=[rpg])
                for kc in range(8):
                    P.op("pe", lambda e, pu=pu, w1=w1, kc=kc, tt=tt: e.matmul(pu[:], lhsT=w1[:, kc, 128:256], rhs=c.xT[:, kc, tt * 512:(tt + 1) * 512], start=(kc == 0), stop=(kc == 7)), reads=[rw1] + xrd, writes=[rpu])
                tA, tB, tC = c.tA[sl], c.tB[sl], c.tC[sl]
                rA, rB, rC = c.r_tA[sl], c.r_tB[sl], c.r_tC[sl]
                bg = c.b1[:, ex * 16 + j:ex * 16 + j + 1]
                bu = c.b1[:, ex * 16 + 8 + j:ex * 16 + 8 + j + 1]
                P.op("dve", lambda e, tA=tA, pg=pg, bg=bg: e.tensor_scalar(out=tA[:], in0=pg[:], scalar1=bg, scalar2=7.0, op0=ALU.add, op1=ALU.min), reads=[rpg, c.r_b1], writes=[rA])
                P.op("act", lambda e, tA=tA, tB=tB: e.activation(out=tB[:], in_=tA[:], func=AF.Sigmoid, scale=1.702), reads=[rA], writes=[rB])
                P.op("act", lambda e, tC=tC, pu=pu, bu=bu: e.activation(out=tC[:], in_=pu[:], func=AF.Identity, bias=bu, scale=1.0), reads=[rpu, c.r_b1], writes=[rC])
                P.op("pool", lambda e, tA=tA, tB=tB: e.tensor_tensor(out=tA[:], in0=tA[:], in1=tB[:], op=ALU.mult), reads=[rA, rB], writes=[rA])
                P.op("pool", lambda e, tC=tC: e.tensor_scalar(out=tC[:], in0=tC[:], scalar1=-7.0, scalar2=7.0, op0=ALU.max, op1=ALU.min), reads=[rC], writes=[rC])
                dst = c.actT[:, j, tt * 512:(tt + 1) * 512]
                P.op("dve", lambda e, tA=tA, tC=tC, dst=dst: e.scalar_tensor_tensor(out=dst, in0=tC[:], scalar=1.0, in1=tA[:], op0=ALU.add, op1=ALU.mult), reads=[rA, rC], writes=[c.r_actT[j][tt]])
        for ts in range(NT):
            for dh in range(2):
                pb = 4 + (n % 4); n += 1
                ps, rps = c.ps[pb], c.r_ps[pb]
                for j in range(8):
                    P.op("pe", lambda e, ps=ps, j=j, ts=ts, dh=dh, w2=w2: e.matmul(ps[:], lhsT=c.actT[:, j, ts * 128:(ts + 1) * 128], rhs=w2[:, j, dh * 512:(dh + 1) * 512], start=(j == 0), stop=(j == 7)), reads=[c.r_actT[j][ts // 4], rw2[j]], writes=[rps])
                xr = c.xres[:, ts, dh * 512:(dh + 1) * 512]
                g = c.G[:, ts, ex:ex + 1]
                P.op("dve", lambda e, ps=ps, xr=xr, g=g: e.scalar_tensor_tensor(out=xr, in0=ps[:], scalar=g, in1=xr, op0=ALU.mult, op1=ALU.add), reads=[rps, c.r_xres[ts], c.r_G[ts]], writes=[c.r_xres[ts]])


_CACHE = {}


def prep_common(inp, ne=32):
    m = {}
    m["ident"] = np.eye(128, dtype=np.float32)
    lnp = np.stack([inp["ln1_g"], inp["ln1_b"], inp["ln2_g"], inp["ln2_b"]], axis=1)
    m["lnp"] = np.ascontiguousarray(np.broadcast_to(lnp[:, :, None, :], (DEPTH, 4, 128, D))).astype(np.float32)
    wr = inp["moe_w_r"][:, :, :ne]
    m["moe_wr"] = np.ascontiguousarray(wr.reshape(DEPTH, 8, 128, ne).transpose(0, 2, 1, 3))
    m["moe_br"] = np.ascontiguousarray(np.broadcast_to(inp["moe_b_r"][:, None, :ne], (DEPTH, 128, ne)))
    w1 = inp["moe_w1"][:, :ne]
    w1r = w1.reshape(DEPTH, ne, 8, 128, 2, 8, 128)
    w1r = w1r.transpose(0, 1, 5, 3, 2, 4, 6)
    m["moe_w1"] = np.ascontiguousarray(w1r).reshape(DEPTH * ne * 8, 128, 8 * 256)
    w2 = inp["moe_w2"][:, :ne]
    w2r = w2.reshape(DEPTH, ne, 8, 128, D).transpose(0, 1, 3, 2, 4)
    m["moe_w2"] = np.ascontiguousarray(w2r).reshape(DEPTH * ne, 128, 8 * 1024)
    b1 = inp["moe_b1"][:, :ne]
    m["moe_b1"] = np.ascontiguousarray(b1.reshape(DEPTH, ne, 16, 128).transpose(0, 3, 1, 2)).reshape(DEPTH, 128, ne * 16)
    m["moe_b2"] = np.ascontiguousarray(inp["moe_b2"][:, :ne])
    return m


def prep_all(inp, ne=32):
    m = prep_common(inp, ne)
    m.update(prep_attn(inp))
    m.update(prep_hyena(inp))
    m.update(prep_even(inp))
    return m


def kernel(**inputs):
    inp = {k: np.asarray(v) for k, v in inputs.items()}
    n_cores = 4
    nseq = inp["x"].shape[0] // n_cores
    nc = build_program(dict(nseq=nseq))
    shared = prep_all(inp)
    x = np.ascontiguousarray(inp["x"], dtype=np.float32)
    in_maps = []
    for i in range(n_cores):
        m = dict(shared)
        m["x"] = x[i * nseq:(i + 1) * nseq]
        in_maps.append(m)
    res = run_bass_kernel_spmd(nc, in_maps, core_ids=list(range(n_cores)))
    return np.concatenate([np.asarray(r["y"]) for r in res.results], axis=0).astype(np.float32)


N_HEADS = 8


def attention(c, j):
    P, nc = c.P, c.nc
    lam_init = 0.8 - 0.6 * math.exp(-0.3 * (2 * j + 1))
    c.phase("attn")
    cv, nr = c.carve, c.newres
    oT = cv([128, 8, S], BF16); r_oT = [[nr("oT") for t in range(NT)] for h in range(8)]
    wo = cv([128, 8, D], BF16); r_wo = nr("wo")
    wh = [cv([128, 8, 384], BF16) for i in range(2)]; r_wh = [nr("wh0"), nr("wh1")]
    qTm = [cv([128, S], BF16) for i in range(2)]; r_q = [[nr("q") for t in range(4)] for i in range(2)]
    r_qz = nr("qz")
    kT = cv([128, S], BF16); r_k = [nr("k") for t in range(4)]
    va = cv([128, NT, 132], BF16); r_v = [nr("v") for t in range(NT)]
    PT = [cv([128, 512], BF16) for i in range(4)]; r_PT = [nr("PT") for i in range(4)]
    LB = cv([128, 8, 2, 128], BF16); RB = cv([128, 2, 512], BF16); DG = cv([128, 8, 128], BF16)
    idb = cv([128, 128], BF16); cbt = cv([128, 8, 31]); r_cst = nr("cst")
    lqk = cv([128, 4, 64]); gs = cv([128, 128]); sm = cv([128, 16]); r_sm = nr("sm")
    fo = [cv([128, 128]) for i in range(4)]; r_fo = [nr("fo") for i in range(4)]
    f2 = [cv([128, 128]) for i in range(4)]; r_f2 = [nr("f2") for i in range(4)]
    fs = [cv([128, 8]) for i in range(4)]; r_fs = [nr("fs") for i in range(4)]
    d = c.attn_d
    P.dma("sp", LB, d["LB"][:, :, :, :], writes=[r_cst])
    P.dma("sp", RB, d["RB"][:, :, :], writes=[r_cst])
    P.dma("sp", DG, d["DG"][:, :, :], writes=[r_cst])
    P.dma("sp", idb, d["idb"][:, :], writes=[r_cst])
    P.dma("sp", cbt, d["cbt"][:, :, :], writes=[r_cst])
    P.dma("sp", lqk, d["lqk"][j], writes=[r_sm])
    P.dma("sp", gs, d["subg"][j], writes=[r_sm])
    ld_a = make_stager(c, 1024, 2)
    for hh in range(8):
        ld_a(wo[:, hh, :], d["wo"][j][:, hh * D:(hh + 1) * D], [r_wo])
    P.o("dve", "tensor_tensor", reads=[r_sm], writes=[r_sm], out=lqk[:, 0, :], in0=lqk[:, 0, :], in1=lqk[:, 1, :], op=ALU.mult)
    P.o("dve", "tensor_tensor", reads=[r_sm], writes=[r_sm], out=lqk[:, 2, :], in0=lqk[:, 2, :], in1=lqk[:, 3, :], op=ALU.mult)
    P.o("dve", "reduce_sum", reads=[r_sm], writes=[r_sm], out=sm[:, 0:1], in_=lqk[:, 0, :], axis=mybir.AxisListType.X)
    P.o("dve", "reduce_sum", reads=[r_sm], writes=[r_sm], out=sm[:, 1:2], in_=lqk[:, 2, :], axis=mybir.AxisListType.X)
    P.o("act", "activation", reads=[r_sm], writes=[r_sm], out=sm[:, 2:4], in_=sm[:, 0:2], func=AF.Exp)
    P.o("dve", "tensor_tensor", reads=[r_sm], writes=[r_sm], out=sm[:, 4:5], in0=sm[:, 3:4], in1=sm[:, 2:3], op=ALU.subtract)
    P.o("dve", "tensor_scalar", reads=[r_sm], writes=[r_sm], out=sm[:, 5:6], in0=sm[:, 4:5], scalar1=-lam_init, scalar2=None, op0=ALU.add)
    P.o("dve", "memset", reads=[], writes=[r_sm], ap=sm[:, 6:7], constant=LN_EPS)
    P.o("dve", "tensor_scalar", reads=[r_sm], writes=[r_sm], out=gs, in0=gs, scalar1=1.0 - lam_init, scalar2=None, op0=ALU.mult)
    mlam = sm[:, 5:6]
    epsa = sm[:, 6:7]
    P.o("pool", "memset", writes=[r_qz], ap=qTm[0][64:128, :], constant=0.0)
    P.o("pool", "memset", writes=[r_qz], ap=qTm[1][0:64, :], constant=0.0)
    for t in range(NT):
        P.o("pool", "memset", writes=[r_v[t]], ap=va[:, t, 128:129], constant=1.0)
    pj = 0
    sn = 0
    pn = 0
    fn = 0
    for h in range(8):
        w, rw = wh[h % 2], r_wh[h % 2]
        for part in range(3):
            ld_a(w[:, :, part * 128:(part + 1) * 128], d["wh"][j * 8 + h].rearrange("p (a t m) -> p a t m", t=3, m=128)[:, :, part, :], [rw], view=lambda v: v.rearrange("p (a b) -> p a b", b=128))
        for tt in range(4):
            xrd = [r for t in range(tt * 4, tt * 4 + 4) for r in c.r_xT[t]]
            cols = slice(tt * 512, (tt + 1) * 512)
            for qk in range(2):
                pb = 6 + pj % 2; pj += 1
                ps, rps = c.ps[pb], c.r_ps[pb]
                for kc in range(8):
                    P.o("pe", "matmul", reads=[rw] + xrd, writes=[rps], out=ps[:], lhsT=w[:, kc, qk * 128:(qk + 1) * 128], rhs=c.xT[:, kc, cols], start=(kc == 0), stop=(kc == 7))
                if qk == 0:
                    P.o("act", "activation", reads=[rps, r_qz], writes=[r_q[0][tt]], out=qTm[0][0:64, cols], in_=ps[0:64, :], func=AF.Copy, scale=0.125)
                    P.o("act", "activation", reads=[rps, r_qz], writes=[r_q[1][tt]], out=qTm[1][64:128, cols], in_=ps[64:128, :], func=AF.Copy, scale=0.125)
                else:
                    P.o("dve", "tensor_copy", reads=[rps], writes=[r_k[tt]], out=kT[:, cols], in_=ps[:])
        for ts in range(NT):
            pb = 6 + pj % 2; pj += 1
            ps, rps = c.ps[pb], c.r_ps[pb]
            for kc in range(8):
                P.o("pe", "matmul", reads=[rw] + list(c.r_xT[ts]), writes=[rps], out=ps[:, 0:128], lhsT=c.xT[:, kc, ts * 128:(ts + 1) * 128], rhs=w[:, kc, 256:384], start=(kc == 0), stop=(kc == 7))
            P.o("act" if ts % 2 == 0 else "dve", "activation" if ts % 2 == 0 else "tensor_copy", reads=[rps], writes=[r_v[ts]], out=va[:, ts, 0:128], in_=ps[:, 0:128], **({"func": AF.Copy} if ts % 2 == 0 else {}))
        for qt in range(4):
            qcols = slice(qt * 512, (qt + 1) * 512)
            for ci in range(2):
                for kt in range(NT):
                    pb = sn % 2; sn += 1
                    ps, rps = c.ps[pb], c.r_ps[pb]
                    nb = min(max(kt - qt * 4, 0), 4)
                    hasd = (qt * 4 <= kt < qt * 4 + 4)
                    ranges = []
                    if nb > 0:
                        ranges.append(("b", 0, nb * 128))
                    if hasd:
                        ranges.append(("d", nb * 128, (nb + 1) * 128))
                    lo_a = (nb + 1) * 128 if hasd else nb * 128
                    if lo_a < 512:
                        ranges.append(("a", lo_a, 512))
                    P.o("pe", "matmul", reads=[r_k[kt // 4], r_q[ci][qt]], writes=[rps], out=ps[:], lhsT=kT[:, kt * 128:(kt + 1) * 128], rhs=qTm[ci][:, qcols], start=True, stop=False)
                    for ri, (kind, lo, hi) in enumerate(ranges):
                        last = (ri == len(ranges) - 1)
                        if kind == "d":
                            P.o("pe", "matmul", reads=[r_cst], writes=[rps], out=ps[:, lo:hi], lhsT=idb, rhs=DG[:, h, :], start=False, stop=last)
                        else:
                            sg = 0 if kind == "a" else 1
                            P.o("pe", "matmul", reads=[r_cst], writes=[rps], out=ps[:, lo:hi], lhsT=LB[:, h, sg, :], rhs=RB[:, sg, lo:hi], start=False, stop=last)
                    pt, rpt = PT[pn % 4], r_PT[pn % 4]; pn += 1
                    dl = qt * 4 - kt
                    for (kind, lo, hi) in ranges:
                        idx = 15 if kind == "d" else (dl + 15 if kind == "a" else -dl + 15)
                        P.o("act", "activation", reads=[rps, r_cst], writes=[rpt], out=pt[:, lo:hi], in_=ps[:, lo:hi], func=AF.Exp, bias=cbt[:, h, idx:idx + 1], scale=1.0)
                    for sub in range(4):
                        acc, racc = c.ps[2 + sub], c.r_ps[2 + sub]
                        P.o("pe", "matmul", reads=[rpt, r_v[kt]], writes=[racc], out=acc[:, 0:129], lhsT=pt[:, sub * 128:(sub + 1) * 128], rhs=va[:, kt, 0:129], start=(kt == 0), stop=(kt == NT - 1))
                for sub in range(4):
                    acc, racc = c.ps[2 + sub], c.r_ps[2 + sub]
                    P.o("dve", "reciprocal", reads=[racc], writes=[r_fs[sub]], out=fs[sub][:, ci:ci + 1], in_=acc[:, 128:129])
                    if ci == 0:
                        P.o("dve", "tensor_scalar", reads=[racc, r_fs[sub]], writes=[r_fo[sub]], out=fo[sub], in0=acc[:, 0:128], scalar1=fs[sub][:, 0:1], scalar2=None, op0=ALU.mult)
                    else:
                        P.o("dve", "tensor_scalar", reads=[racc, r_fs[sub], r_sm], writes=[r_f2[sub]], out=f2[sub], in0=acc[:, 0:128], scalar1=fs[sub][:, 1:2], scalar2=mlam, op0=ALU.mult, op1=ALU.mult)
            for sub in range(4):
                ts = qt * 4 + sub
                b = sub
                P.o("pool", "tensor_tensor", reads=[r_fo[b], r_f2[b]], writes=[r_fo[b]], out=fo[b], in0=fo[b], in1=f2[b], op=ALU.add)
                P.o("act", "activation", reads=[r_fo[b]], writes=[r_f2[b]], out=f2[b], in_=fo[b], func=AF.Square)
                P.o("dve", "reduce_sum", reads=[r_f2[b]], writes=[r_fs[b]], out=fs[b][:, 2:3], in_=f2[b], axis=mybir.AxisListType.X)
                P.o("act", "activation", reads=[r_fs[b], r_sm], writes=[r_fs[b]], out=fs[b][:, 3:4], in_=fs[b][:, 2:3], func=AF.Sqrt, bias=epsa, scale=1.0 / 128.0)
                P.o("dve", "reciprocal", reads=[r_fs[b]], writes=[r_fs[b]], out=fs[b][:, 4:5], in_=fs[b][:, 3:4])
                P.o("dve", "scalar_tensor_tensor", reads=[r_fo[b], r_fs[b], r_sm], writes=[r_fo[b]], out=fo[b], in0=fo[b], scalar=fs[b][:, 4:5], in1=gs, op0=ALU.mult, op1=ALU.mult)
                pb = 6 + pj % 2; pj += 1
                ps, rps = c.ps[pb], c.r_ps[pb]
                P.o("pe", "transpose", reads=[r_fo[b], c.r_ident], writes=[rps], out=ps[:, 0:128], in_=fo[b], identity=c.ident[:])
                P.o("act", "activation", reads=[rps], writes=[r_oT[h][ts]], out=oT[:, h, ts * 128:(ts + 1) * 128], in_=ps[:, 0:128], func=AF.Copy)
    c.dbg("sm", sm, [128, 16], F32, [r_sm])
    c.dbg("kT", kT, [128, S], BF16, r_k)
    c.dbg("q0", qTm[0], [128, S], BF16, r_q[0] + [r_qz])
    c.dbg("va", va, [128, NT, 132], BF16, r_v)
    c.dbg("oT", oT, [128, 8, S], BF16, [r for hh in r_oT for r in hh])
    c.dbg("fo", fo[3], [128, 128], F32, [r_fo[3]])
    c.dbg("fs", fs[3], [128, 8], F32, [r_fs[3]])
    c.dbg("pt", PT[(pn - 1) % 4], [128, 512], BF16, [r_PT[(pn - 1) % 4]])
    n = 0
    for ts in range(NT):
        for dh in range(2):
            pb = 6 + n % 2; n += 1
            ps, rps = c.ps[pb], c.r_ps[pb]
            for h in range(8):
                P.o("pe", "matmul", reads=[r_oT[h][ts], r_wo], writes=[rps], out=ps[:], lhsT=oT[:, h, ts * 128:(ts + 1) * 128], rhs=wo[:, h, dh * 512:(dh + 1) * 512], start=(h == 0), stop=(h == 7))
            xr = c.xres[:, ts, dh * 512:(dh + 1) * 512]
            P.o("dve", "scalar_tensor_tensor", reads=[rps, c.r_xres[ts]], writes=[c.r_xres[ts]], out=xr, in0=xr, scalar=ALPHA, in1=ps[:], op0=ALU.mult, op1=ALU.add)


def attn_consts():
    sl = np.array([2.0 ** (-(h + 1)) for h in range(8)], dtype=np.float64)
    LB = np.zeros((128, 8, 2, 128), np.float64)
    m = np.arange(128)
    for h in range(8):
        LB[0, h, 0, :] = sl[h] * m
        LB[0, h, 1, :] = -sl[h] * m
        LB[1, h, :, :] = sl[h]
        LB[2, h, :, :] = sl[h]
    RB = np.zeros((128, 2, 512), np.float64)
    n = np.arange(512)
    RB[0, :, :] = 1.0
    RB[1, 0, :] = -256.0 * (n // 256); RB[1, 1, :] = 256.0 * (n // 256)
    RB[2, 0, :] = -(n % 256); RB[2, 1, :] = (n % 256)
    DG = np.zeros((128, 8, 128), np.float64)
    for h in range(8):
        DG[:, h, :] = -sl[h] * np.abs(m[None, :] - m[:, None])
    cbt = np.zeros((128, 8, 31), np.float64)
    for h in range(8):
        cbt[:, h, :] = -sl[h] * 128.0 * (np.arange(31) - 15)
    bf = ml_dtypes.bfloat16
    return {"att_LB": LB.astype(bf), "att_RB": RB.astype(bf), "att_DG": DG.astype(bf),
            "att_idb": np.eye(128).astype(bf), "att_cbt": cbt.astype(np.float32)}


def prep_attn(inp):
    m = attn_consts()
    wqkv = inp["attn_w_qkv"]
    n = wqkv.shape[0]
    q = wqkv[:, :, 0:1024].reshape(n, 8, 128, 8, 128)
    k = wqkv[:, :, 1024:2048].reshape(n, 8, 128, 8, 128)
    v = wqkv[:, :, 2048:3072].reshape(n, 8, 128, 8, 128)
    wh = np.stack([q, k, v], axis=4)
    wh = wh.transpose(0, 3, 2, 1, 4, 5)
    m["att_wh"] = np.ascontiguousarray(wh).reshape(n * 8, 128, 8 * 384)
    wo = inp["attn_w_out"].reshape(n, 8, 128, D).transpose(0, 2, 1, 3)
    m["att_wo"] = np.ascontiguousarray(wo).reshape(n, 128, 8 * D)
    lqk = np.stack([inp["attn_lq1"], inp["attn_lk1"], inp["attn_lq2"], inp["attn_lk2"]], axis=1)
    m["att_lqk"] = np.ascontiguousarray(np.broadcast_to(lqk[:, None], (n, 128, 4, 64)))
    m["att_subg"] = np.ascontiguousarray(np.broadcast_to(inp["attn_subln_g"][:, None, :], (n, 128, 128)))
    return m


TWO_PI = 2.0 * math.pi
MAGIC = 12582912.0


def hyena_consts():
    L, N = S, 2 * S
    f64 = np.float64
    pos = np.arange(L, dtype=np.float32)
    t = np.linspace(0.0, 1.0, L, dtype=np.float32)[:, None]
    bands = 16
    f = np.linspace(1e-4, bands - 1, bands, dtype=np.float32)
    ang = (np.float32(2.0 * math.pi / L) * pos[:, None] * f[None, :]).astype(np.float32)
    feats = np.concatenate([t, np.cos(ang), -np.sin(ang)], axis=-1).astype(np.float32)
    max_decay = math.log(1e-2) / 0.3
    min_decay = math.log(1e-2) / 1.5
    deltas = np.abs(np.linspace(min_decay, max_decay, 512, dtype=np.float32))
    decay = np.exp(-t * deltas[None, :]).astype(np.float32)
    s = np.arange(L, dtype=f64)
    angm = 2.0 * math.pi * np.outer(s, s) / N
    FcT = np.cos(angm)
    FsT = -np.sin(angm)
    FsT[:, 0] = (-1.0) ** s
    Gc = (2.0 / N) * np.cos(angm)
    Gc[0, :] = 1.0 / N
    Gs = -(2.0 / N) * np.sin(angm)
    Gs[0, :] = (1.0 / N) * (-1.0) ** s
    bf = ml_dtypes.bfloat16

    def blk(A, B):
        X = np.stack([A, B], axis=0).reshape(2, 16, 128, 16, 128)
        return np.ascontiguousarray(X.transpose(3, 2, 1, 0, 4)).astype(bf).reshape(16, 128, 16 * 2 * 128)

    sgn = -np.ones((128, 16), np.float32)
    sgn[0, 0] = 1.0
    return {"hy_featsT": np.ascontiguousarray(feats.T), "hy_decay": np.ascontiguousarray(decay.reshape(16, 128, 512).transpose(1, 0, 2)),
            "hy_F": blk(FcT, FsT), "hy_G": blk(Gc, Gs), "hy_sgn": sgn}


def hyena_prologue(c):
    P, nc = c.P, c.nc
    d = c.hy_d
    for j in range(2):
        if (2 * j) not in c.cfg.get("layers", list(range(DEPTH))):
            continue
        c.phase("hyfilt")
        cv, nr = c.carve, c.newres
        Tf = [cv([128, 16, 512], BF16) for g in range(4)]; r_Tf = [[nr("Tf") for t in range(16)] for g in range(4)]
        keepf = c.ov_off
        featsT = cv([33, S]); r_ft = nr("ft")
        f1w = cv([33, 64]); f2w = cv([64, 64]); f3w = cv([64, 2048]); fvec = cv([64, 8]); r_fw = nr("fw")
        xTf32 = c.xT[:].rearrange("p a b -> p (a b)").bitcast(F32)
        h1 = xTf32[0:64, 0:S]; r_h1 = nr("h1")
        h2 = xTf32[0:64, S:2 * S]; r_h2 = nr("h2")
        tmp = [cv([64, 512]) for i in range(2)]; r_tmp = [nr("t0"), nr("t1")]
        tk = [cv([64, 512]) for i in range(2)]; r_tk = [nr("k0"), nr("k1")]
        dec = [cv([128, 512]) for i in range(2)]; r_dec = [nr("d0"), nr("d1")]
        P.dma("sp", featsT, d["featsT"][:, :], writes=[r_ft])
        P.dma("sp", f1w, d["f1w"][j], writes=[r_fw])
        P.dma("sp", f2w, d["f2w"][j], writes=[r_fw])
        P.dma("sp", f3w, d["f3w"][j], writes=[r_fw])
        P.dma("sp", fvec, d["fvec"][j], writes=[r_fw])

        def sin_layer(src_fn, K, bcol, fcol, dst, rdst, rsrc):
            for nt in range(4):
                b = nt % 2
                ps, rps = c.ps[b], c.r_ps[b]
                src_fn(ps, rps, nt)
                tm, rtm, kk, rkk = tmp[b], r_tmp[b], tk[b], r_tk[b]
                P.o("dve", "tensor_scalar", reads=[rps, r_fw], writes=[rtm], out=tm, in0=ps[0:64, :], scalar1=fvec[:, bcol:bcol + 1], scalar2=fvec[:, fcol:fcol + 1], op0=ALU.add, op1=ALU.mult)
                P.o("dve", "tensor_scalar", reads=[rtm], writes=[rkk], out=kk, in0=tm, scalar1=1.0 / TWO_PI, scalar2=MAGIC, op0=ALU.mult, op1=ALU.add)
                P.o("dve", "tensor_scalar", reads=[rkk], writes=[rkk], out=kk, in0=kk, scalar1=-MAGIC, scalar2=None, op0=ALU.add)
                P.o("dve", "scalar_tensor_tensor", reads=[rkk, rtm], writes=[rtm], out=tm, in0=kk, scalar=-TWO_PI, in1=tm, op0=ALU.mult, op1=ALU.add)
                P.o("dve", "tensor_scalar", reads=[rtm], writes=[rtm], out=tm, in0=tm, scalar1=-3.141592, scalar2=3.141592, op0=ALU.max, op1=ALU.min)
                P.o("act", "activation", reads=[rtm], writes=[rdst], out=dst[:, nt * 512:(nt + 1) * 512], in_=tm, func=AF.Sin)

        def src1(ps, rps, nt):
            P.o("pe", "matmul", reads=[r_fw, r_ft], writes=[rps], out=ps[0:64, :], lhsT=f1w, rhs=featsT[:, nt * 512:(nt + 1) * 512], start=True, stop=True)

        def src2(ps, rps, nt):
            P.o("pe", "matmul", reads=[r_fw, r_h1], writes=[rps], out=ps[0:64, :], lhsT=f2w, rhs=h1[:, nt * 512:(nt + 1) * 512], start=True, stop=True)

        sin_layer(src1, 33, 0, 1, h1, r_h1, None)
        sin_layer(src2, 64, 2, 3, h2, r_h2, None)
        n = 0
        for nt in range(16):
            db, rdb = dec[nt % 2], r_dec[nt % 2]
            P.dma("sp", db, d["decay"][:, nt, :], writes=[rdb])
            for g in range(4):
                pb = 2 + n % 2; n += 1
                ps, rps = c.ps[pb], c.r_ps[pb]
                P.o("pe", "matmul", reads=[r_h2, r_fw], writes=[rps], out=ps[:], lhsT=h2[:, nt * 128:(nt + 1) * 128], rhs=f3w[:, g * 512:(g + 1) * 512], start=True, stop=True)
                P.o("dve", "tensor_tensor", reads=[rps, rdb], writes=[r_Tf[g][nt]], out=Tf[g][:, nt, :], in0=ps[:], in1=db, op=ALU.mult)
        for g in (2, 3):
            P.o("dve", "memset", writes=[r_Tf[g][0]], ap=Tf[g][0:1, 0, :], constant=0.0)
        c.phase("hyspec"); c.ov_off = keepf
        Fb = [cv([128, 16, 2, 128], BF16) for i in range(2)]; r_Fb = [nr("F0"), nr("F1")]
        skb = cv([128, 2, 512]); sgn = cv([128, 16]); r_sk = nr("sk")
        ko = [cv([128, 2, 512]) for i in range(2)]; r_ko = [nr("ko0"), nr("ko1")]
        pt = [cv([128, 512]) for i in range(2)]; r_pt = [nr("pt0"), nr("pt1")]
        P.dma("sp", skb, d["skip"][j], writes=[r_sk])
        P.dma("sp", sgn, d["sgn"][:, :], writes=[r_sk])
        n = 0
        for fi in range(16):
            fb, rfb = Fb[fi % 2], r_Fb[fi % 2]
            P.dma("sp", fb, d["F"][fi].rearrange("p (a b m) -> p a b m", b=2, m=128), writes=[rfb])
            for o in range(2):
                kb, rkb = ko[(fi * 2 + o) % 2], r_ko[(fi * 2 + o) % 2]
                for cs in range(2):
                    p1, rp1 = c.ps[4 + 2 * (n % 2)], c.r_ps[4 + 2 * (n % 2)]
                    p2, rp2 = c.ps[5 + 2 * (n % 2)], c.r_ps[5 + 2 * (n % 2)]
                    n += 1
                    for sc in range(16):
                        P.o("pe", "matmul", reads=[rfb, r_Tf[o][sc]], writes=[rp1], out=p1[:], lhsT=fb[:, sc, cs, :], rhs=Tf[o][:, sc, :], start=(sc == 0), stop=(sc == 15))
                    for sc in range(16):
                        P.o("pe", "matmul", reads=[rfb, r_Tf[2 + o][sc]], writes=[rp2], out=p2[:], lhsT=fb[:, sc, cs, :], rhs=Tf[2 + o][:, sc, :], start=(sc == 0), stop=(sc == 15))
                    tb, rtb = pt[cs], r_pt[cs]
                    P.o("act", "activation", reads=[rp1], writes=[rtb], out=tb, in_=p1[:], func=AF.Copy)
                    if cs == 0:
                        P.o("dve", "tensor_tensor", reads=[rp2, rtb], writes=[rtb], out=tb, in0=p2[:], in1=tb, op=ALU.add)
                        P.o("pool", "tensor_tensor", reads=[rtb, r_sk], writes=[rkb], out=kb[:, 0, :], in0=tb, in1=skb[:, o, :], op=ALU.add)
                    else:
                        P.o("dve", "scalar_tensor_tensor", reads=[rp2, rtb, r_sk], writes=[rkb], out=kb[:, 1, :], in0=p2[:], scalar=sgn[:, fi:fi + 1], in1=tb, op0=ALU.mult, op1=ALU.add)
                        if fi == 0:
                            P.o("dve", "tensor_tensor", reads=[rkb, r_sk], writes=[rkb], out=kb[0:1, 1, :], in0=kb[0:1, 1, :], in1=skb[0:1, o, :], op=ALU.add)
                P.dma("sp", d["KF"][(j * 2 + o) * 16 + fi], kb, reads=[rkb], writes=[c.r_KF[j][o][fi]])
        c.touch_all([r for t in range(NT) for r in c.r_xT[t]])
        if c.cfg.get("debug_kf"):
            c.dbg("h1_%d" % j, h1, [64, S], F32, [r_h1])
            c.dbg("h2_%d" % j, h2, [64, S], F32, [r_h2])
            c.dbg("Tf0_%d" % j, Tf[0], [128, 16, 512], BF16, r_Tf[0])


def prep_hyena(inp):
    m = hyena_consts()
    n = inp["hy_f1_w"].shape[0]
    m["hy_f1w"] = np.ascontiguousarray(inp["hy_f1_w"])
    m["hy_f2w"] = np.ascontiguousarray(inp["hy_f2_w"])
    m["hy_f3w"] = np.ascontiguousarray(inp["hy_f3_w"])
    fv = np.zeros((n, 64, 8), np.float32)
    fv[:, :, 0] = inp["hy_f1_b"]; fv[:, :, 1] = inp["hy_f1_freq"]; fv[:, :, 2] = inp["hy_f2_b"]; fv[:, :, 3] = inp["hy_f2_freq"]
    m["hy_fvec"] = fv
    m["hy_skip"] = np.ascontiguousarray(np.broadcast_to(inp["hy_skip"][:, None], (n, 128, 2, 512)))
    return m


EV_BIN, EV_SW, EV_SB, EV_DW, EV_DB, EV_LG, EV_LB, EV_N = 0, 20, 56, 68, 192, 196, 200, 204


def prep_even(inp):
    m = {}
    n = inp["hy_cf_w_in"].shape[0]
    w = inp["hy_cf_w_in"].reshape(n, 8, 128, 20, 128).transpose(0, 3, 2, 1, 4)
    m["ev_win"] = np.ascontiguousarray(w).reshape(n * 20, 128, 8 * 128)
    evv = np.zeros((n, 128, EV_N), np.float32)
    evv[:, :, EV_BIN:EV_BIN + 20] = inp["hy_cf_b_in"].reshape(n, 20, 128).transpose(0, 2, 1)
    evv[:, :, EV_SW:EV_SW + 36] = inp["hy_short_w"].reshape(n, 3, 12, 128).transpose(0, 3, 2, 1).reshape(n, 128, 36)
    evv[:, :, EV_SB:EV_SB + 12] = inp["hy_short_b"].reshape(n, 12, 128).transpose(0, 2, 1)
    evv[:, :, EV_DW:EV_DW + 124] = inp["cf_dw_w"].reshape(n, 31, 4, 128).transpose(0, 3, 2, 1).reshape(n, 128, 124)
    evv[:, :, EV_DB:EV_DB + 4] = inp["cf_dw_b"].reshape(n, 4, 128).transpose(0, 2, 1)
    evv[:, :, EV_LG:EV_LG + 4] = inp["cf_ln_g"].reshape(n, 4, 128).transpose(0, 2, 1)
    evv[:, :, EV_LB:EV_LB + 4] = inp["cf_ln_b"].reshape(n, 4, 128).transpose(0, 2, 1)
    m["ev_vec"] = evv
    wo = inp["even_w_out"].reshape(n, 8, 128, D).transpose(0, 2, 1, 3)
    m["ev_wo"] = np.ascontiguousarray(wo).reshape(n, 128, 8 * D)
    m["ev_bo"] = np.ascontiguousarray(inp["even_b_out"].reshape(n, 1, D))
    m["ones128"] = np.ones((128, 128), np.float32)
    return m


def even_mixer(c, j):
    P, nc = c.P, c.nc
    d = c.ev_d
    hd = c.hy_d
    X1, X2 = d["X12"][0], d["X12"][1]
    c.phase("even1")
    cv, nr = c.carve, c.newres
    zuT = cv([128, 8, S], BF16); r_zu = [[nr("zu") for t in range(NT)] for k in range(8)]
    evv = cv([128, EV_N]); r_ev = nr("evv")
    ones = cv([128, 128]); r_on = nr("ones")
    keep = c.ov_off
    P.dma("sp", evv, d["vec"][j], writes=[r_ev])
    P.dma("sp", ones, d["ones"][:, :], writes=[r_on])
    wch = [cv([128, 8, 128], BF16) for i in range(3)]; r_wch = [nr("wch") for i in range(3)]
    wn = [0]

    ldst = [make_stager(c, 1024, 2)]

    def load_w(ch):
        i = wn[0] % 3; wn[0] += 1
        ldst[0](wch[i], d["win"][j * 20 + ch].rearrange("p (a b) -> p a b", b=128), [r_wch[i]], view=lambda v: v.rearrange("p (a b) -> p a b", b=128))
        return wch[i], r_wch[i]

    pbn = [0]

    def proj(ch, evac):
        w, rw = load_w(ch)
        for tt in range(4):
            pb = pbn[0] % 2; pbn[0] += 1
            ps, rps = c.ps[pb], c.r_ps[pb]
            xrd = [r for t in range(tt * 4, tt * 4 + 4) for r in c.r_xT[t]]
            for kc in range(8):
                P.o("pe", "matmul", reads=[rw] + xrd, writes=[rps], out=ps[:], lhsT=w[:, kc, :], rhs=c.xT[:, kc, tt * 512:(tt + 1) * 512], start=(kc == 0), stop=(kc == 7))
            evac(ps, rps, tt)

    acc = [cv([128, S]) for i in range(4)]; r_acc = [[nr("acc") for t in range(4)] for i in range(4)]
    off_ub = c.ov_off
    ub = [cv([128, S]) for i in range(2)]; r_ub = [[nr("ub") for t in range(4)] for i in range(2)]
    sgb = [cv([128, 512]) for i in range(2)]; r_sgb = [nr("sg0"), nr("sg1")]
    sn = [0]
    for cc in range(4):
        u, ru = ub[cc % 2], r_ub[cc % 2]

        def evac_a(ps, rps, tt, u=u, ru=ru, cc=cc):
            P.o("act", "activation", reads=[rps, r_ev], writes=[ru[tt]], out=u[:, tt * 512:(tt + 1) * 512], in_=ps[:], func=AF.Identity, bias=evv[:, EV_BIN + 12 + cc:EV_BIN + 13 + cc], scale=1.0)

        def evac_g(ps, rps, tt, u=u, ru=ru, cc=cc):
            i = sn[0] % 2; sn[0] += 1
            P.o("act", "activation", reads=[rps, r_ev], writes=[r_sgb[i]], out=sgb[i], in_=ps[:], func=AF.Sigmoid, bias=evv[:, EV_BIN + 16 + cc:EV_BIN + 17 + cc], scale=1.0)
            P.o("pool", "tensor_tensor", reads=[r_sgb[i], ru[tt]], writes=[ru[tt]], out=u[:, tt * 512:(tt + 1) * 512], in0=u[:, tt * 512:(tt + 1) * 512], in1=sgb[i], op=ALU.mult)

        proj(12 + cc, evac_a)
        proj(16 + cc, evac_g)
        a, ra = acc[cc], r_acc[cc]
        wcol = lambda k, cc=cc: evv[:, EV_DW + cc * 31 + k:EV_DW + cc * 31 + k + 1]
        P.o("dve", "tensor_scalar", reads=list(ru) + [r_ev], writes=list(ra), out=a, in0=u, scalar1=wcol(15), scalar2=evv[:, EV_DB + cc:EV_DB + cc + 1], op0=ALU.mult, op1=ALU.add)
        for k in range(31):
            if k == 15:
                continue
            sft = k - 15
            lo, hi = max(0, -sft), min(S, S - sft)
            P.o("dve", "scalar_tensor_tensor", reads=list(ru) + [r_ev], writes=list(ra), out=a[:, lo:hi], in0=u[:, lo + sft:hi + sft], scalar=wcol(k), in1=a[:, lo:hi], op0=ALU.mult, op1=ALU.add)
    c.phase("even1b"); c.ov_off = off_ub
    sq = [cv([128, 512]) for i in range(2)]; r_sq = [nr("sq0"), nr("sq1")]
    mb = [cv([128, 512]) for i in range(2)]; r_mb = [nr("mb0"), nr("mb1")]
    rb = [cv([128, 512]) for i in range(2)]; r_rb = [nr("rb0"), nr("rb1")]
    zt = [cv([128, 512]) for i in range(2)]; r_zt = [nr("zt0"), nr("zt1")]
    eps_t = cv([128, 1]); r_e = nr("eps")
    P.o("dve", "memset", writes=[r_e], ap=eps_t, constant=LN_EPS)
    qn = 0
    for tt in range(4):
        cols = slice(tt * 512, (tt + 1) * 512)
        b = tt % 2
        p_s, rp_s = c.ps[2 + 2 * b], c.r_ps[2 + 2 * b]
        p_q, rp_q = c.ps[3 + 2 * b], c.r_ps[3 + 2 * b]
        for cc in range(4):
            P.o("pe", "matmul", reads=[r_on, r_acc[cc][tt]], writes=[rp_s], out=p_s[:], lhsT=ones, rhs=acc[cc][:, cols], start=(cc == 0), stop=(cc == 3))
        for cc in range(4):
            i = qn % 2; qn += 1
            P.o("act", "activation", reads=[r_acc[cc][tt]], writes=[r_sq[i]], out=sq[i], in_=acc[cc][:, cols], func=AF.Square)
            P.o("pe", "matmul", reads=[r_on, r_sq[i]], writes=[rp_q], out=p_q[:], lhsT=ones, rhs=sq[i], start=(cc == 0), stop=(cc == 3))
        m_, rm_, r_, rr_ = mb[b], r_mb[b], rb[b], r_rb[b]
        P.o("act", "activation", reads=[rp_s], writes=[rm_], out=m_, in_=p_s[:], func=AF.Copy, scale=1.0 / 512.0)
        P.o("dve", "tensor_tensor", reads=[rm_], writes=[rr_], out=r_, in0=m_, in1=m_, op=ALU.mult)
        P.o("dve", "scalar_tensor_tensor", reads=[rp_q, rr_], writes=[rr_], out=r_, in0=p_q[:], scalar=1.0 / 512.0, in1=r_, op0=ALU.mult, op1=ALU.subtract)
        P.o("act", "activation", reads=[rr_, r_e], writes=[rr_], out=r_, in_=r_, func=AF.Sqrt, bias=eps_t, scale=1.0)
        P.o("dve", "reciprocal", reads=[rr_], writes=[rr_], out=r_, in_=r_)
        for cc in range(4):
            i = qn % 2; qn += 1
            z, rz = zt[i], r_zt[i]
            a = acc[cc][:, cols]
            P.o("pool", "tensor_tensor", reads=[r_acc[cc][tt], rm_], writes=[r_acc[cc][tt]], out=a, in0=a, in1=m_, op=ALU.subtract)
            P.o("dve", "tensor_tensor", reads=[r_acc[cc][tt], rr_], writes=[r_acc[cc][tt]], out=a, in0=a, in1=r_, op=ALU.mult)
            P.o("act", "activation", reads=[r_acc[cc][tt], r_ev], writes=[r_acc[cc][tt]], out=a, in_=a, func=AF.Identity, bias=evv[:, EV_LB + cc:EV_LB + cc + 1], scale=evv[:, EV_LG + cc:EV_LG + cc + 1])
            P.o("act", "activation", reads=[r_acc[cc][tt]], writes=[rz], out=z, in_=a, func=AF.Sigmoid)
            P.o("pool", "tensor_tensor", reads=[r_acc[cc][tt], rz], writes=[r_zu[4 + cc][4 * tt + q] for q in range(4)], out=zuT[:, 4 + cc, cols], in0=a, in1=z, op=ALU.mult)
    c.phase("even2"); c.ov_off = keep
    V = cv([128, NT, 512], BF16); r_V = [nr("V") for t in range(NT)]
    keep2 = c.ov_off
    wch = [cv([128, 8, 128], BF16) for i in range(3)]; r_wch = [nr("wch") for i in range(3)]
    ldst[0] = make_stager(c, 1024, 2)
    pr = [cv([128, S]) for i in range(2)]; r_pr = [[nr("pr") for t in range(4)] for i in range(2)]
    hy = [cv([128, S]) for i in range(2)]; r_hy = [nr("hy0"), nr("hy1")]
    xs = [cv([128, NT, 128], BF16) for i in range(2)]; r_xs = [nr("xs0"), nr("xs1")]
    tn = 0
    for ch in range(12):
        p_, rp_ = pr[ch % 2], r_pr[ch % 2]
        h_, rh_ = hy[ch % 2], r_hy[ch % 2]

        def evac_h(ps, rps, tt, p_=p_, rp_=rp_, ch=ch):
            P.o("act", "activation", reads=[rps, r_ev], writes=[rp_[tt]], out=p_[:, tt * 512:(tt + 1) * 512], in_=ps[:], func=AF.Identity, bias=evv[:, EV_BIN + ch:EV_BIN + ch + 1], scale=1.0)

        proj(ch, evac_h)
        w0, w1_, w2_ = [evv[:, EV_SW + ch * 3 + k:EV_SW + ch * 3 + k + 1] for k in range(3)]
        P.o("pool", "tensor_scalar", reads=list(rp_) + [r_ev], writes=[rh_], out=h_, in0=p_, scalar1=w1_, scalar2=evv[:, EV_SB + ch:EV_SB + ch + 1], op0=ALU.mult, op1=ALU.add)
        P.o("dve", "scalar_tensor_tensor", reads=list(rp_) + [r_ev, rh_], writes=[rh_], out=h_[:, 1:S], in0=p_[:, 0:S - 1], scalar=w0, in1=h_[:, 1:S], op0=ALU.mult, op1=ALU.add)
        P.o("dve", "scalar_tensor_tensor", reads=list(rp_) + [r_ev, rh_], writes=[rh_], out=h_[:, 0:S - 1], in0=p_[:, 1:S], scalar=w2_, in1=h_[:, 0:S - 1], op0=ALU.mult, op1=ALU.add)
        which, cc = ch // 4, ch % 4
        if which < 2:
            x_, rx_ = xs[ch % 2], r_xs[ch % 2]
        for tg in range(4):
            pb = 4 + tn % 4; tn += 1
            ps, rps = c.ps[pb], c.r_ps[pb]
            for q in range(4):
                t = tg * 4 + q
                P.o("pe", "transpose", reads=[rh_, c.r_ident], writes=[rps], out=ps[:, q * 128:(q + 1) * 128], in_=h_[:, t * 128:(t + 1) * 128], identity=c.ident[:])
            src = ps[:].rearrange("p (a b) -> p a b", b=128)
            eng = "act" if tg % 2 == 0 else "dve"
            if which == 2:
                dst, wr_ = V[:, tg * 4:(tg + 1) * 4, cc * 128:(cc + 1) * 128], [r_V[tg * 4 + q] for q in range(4)]
            else:
                dst, wr_ = x_[:, tg * 4:(tg + 1) * 4, :], [rx_]
            if eng == "act":
                P.o("act", "activation", reads=[rps], writes=wr_, out=dst, in_=src, func=AF.Copy)
            else:
                P.o("dve", "tensor_copy", reads=[rps], writes=wr_, out=dst, in_=src)
        if which < 2:
            P.dma("sp", d["X12"][which][cc], x_[:].rearrange("p a b -> p (a b)"), reads=[rx_], writes=[c.r_X12[which][cc]])
    c.phase("even3"); c.ov_off = keep2
    Y = c.xT[:].rearrange("p a b -> p (a b)").rearrange("p (f s n) -> p f s n", s=2, n=512)
    r_Y = [nr("Y") for f in range(16)]
    wo = cv([128, 8, D], BF16); r_wo = nr("wo")
    bo = cv([1, D]); r_bo = nr("bo")
    ld_e = make_stager(c, 1024, 1)
    for kc in range(8):
        ld_e(wo[:, kc, :], d["wo"][j][:, kc * D:(kc + 1) * D], [r_wo])
    P.dma("sp", bo, d["bo"][j], writes=[r_bo])
    fg = [cv([128, 16, 2, 128], BF16) for i in range(2)]; r_fg = [nr("fg0"), nr("fg1")]
    kb = [cv([128, 2, 512]) for i in range(1)]; r_kb = [nr("kb0")]
    tq = [[cv([128, 512]) for k in range(4)] for i in range(1)]; r_tq = [[nr("tq") for k in range(4)] for i in range(1)]
    xg = [cv([128, 4, 128], BF16) for i in range(2)]; r_xg = [nr("xg0"), nr("xg1")]
    z2 = [cv([128, 512]) for i in range(1)]; r_z2 = [nr("z20")]
    fgn = 0
    for o in range(2):
        for fi in range(16):
            f_, rf_ = fg[fgn % 2], r_fg[fgn % 2]; fgn += 1
            P.dma("sp", f_, hd["F"][fi].rearrange("p (a b m) -> p a b m", b=2, m=128), writes=[rf_])
            k_, rk_ = kb[0], r_kb[0]
            P.dma("sp", k_, hd["KF"][(j * 2 + o) * 16 + fi], reads=[c.r_KF[j][o][fi]], writes=[rk_])
            b = fi % 2
            zc, rzc, zs, rzs = c.ps[2 * b], c.r_ps[2 * b], c.ps[2 * b + 1], c.r_ps[2 * b + 1]
            for cs, (pz, rpz) in enumerate(((zc, rzc), (zs, rzs))):
                for sc in range(16):
                    P.o("pe", "matmul", reads=[rf_, r_V[sc]], writes=[rpz], out=pz[:], lhsT=f_[:, sc, cs, :], rhs=V[:, sc, :], start=(sc == 0), stop=(sc == 15))
            t_, rt_ = tq[0], r_tq[0]
            P.o("dve", "tensor_tensor", reads=[rzc, rk_], writes=[rt_[0]], out=t_[0], in0=zc[:], in1=k_[:, 0, :], op=ALU.mult)
            P.o("dve", "tensor_tensor", reads=[rzs, rk_], writes=[rt_[1]], out=t_[1], in0=zs[:], in1=k_[:, 1, :], op=ALU.mult)
            P.o("dve", "tensor_tensor", reads=[rzc, rk_], writes=[rt_[2]], out=t_[2], in0=zc[:], in1=k_[:, 1, :], op=ALU.mult)
            P.o("dve", "tensor_tensor", reads=[rzs, rk_], writes=[rt_[3]], out=t_[3], in0=zs[:], in1=k_[:, 0, :], op=ALU.mult)
            P.o("pool", "tensor_tensor", reads=[rt_[0], rt_[1]], writes=[r_Y[fi]], out=Y[:, fi, 0, :], in0=t_[0], in1=t_[1], op=ALU.subtract)
            P.o("pool", "tensor_tensor", reads=[rt_[2], rt_[3]], writes=[r_Y[fi]], out=Y[:, fi, 1, :], in0=t_[2], in1=t_[3], op=ALU.add)
            if fi == 0:
                P.o("pool", "tensor_copy", reads=[rt_[0]], writes=[r_Y[fi]], out=Y[0:1, 0, 0, :], in_=t_[0][0:1, :])
                P.o("pool", "tensor_copy", reads=[rt_[1]], writes=[r_Y[fi]], out=Y[0:1, 0, 1, :], in_=t_[1][0:1, :])
        for tc in range(16):
            g_, rg_ = fg[fgn % 2], r_fg[fgn % 2]; fgn += 1
            P.dma("sp", g_, hd["G"][tc].rearrange("p (a b m) -> p a b m", b=2, m=128), writes=[rg_])
            x_, rx_ = xg[tc % 2], r_xg[tc % 2]
            P.dma("sp", x_, d["X12"][o][:, :, tc * 128:(tc + 1) * 128].rearrange("c p m -> p c m"), reads=list(c.r_X12[o]), writes=[rx_])
            pb = 4 + tc % 2
            ps, rps = c.ps[pb], c.r_ps[pb]
            for fi in range(16):
                for cs in range(2):
                    P.o("pe", "matmul", reads=[rg_, r_Y[fi]], writes=[rps], out=ps[:], lhsT=g_[:, fi, cs, :], rhs=Y[:, fi, cs, :], start=(fi == 0 and cs == 0), stop=(fi == 15 and cs == 1))
            xgv = x_[:].rearrange("p a b -> p (a b)")
            if o == 0:
                P.o("dve", "tensor_tensor", reads=[rps, rx_], writes=[r_V[tc]], out=V[:, tc, :], in0=ps[:], in1=xgv, op=ALU.mult)
            else:
                z_, rz_ = z2[0], r_z2[0]
                P.o("dve", "tensor_tensor", reads=[rps, rx_], writes=[rz_], out=z_, in0=ps[:], in1=xgv, op=ALU.mult)
                pt_, rpt_ = c.ps[6 + tc % 2], c.r_ps[6 + tc % 2]
                for cc in range(4):
                    P.o("pe", "transpose", reads=[rz_, c.r_ident], writes=[rpt_], out=pt_[:, cc * 128:(cc + 1) * 128], in_=z_[:, cc * 128:(cc + 1) * 128], identity=c.ident[:])
                P.o("act", "activation", reads=[rpt_], writes=[r_zu[cc][tc] for cc in range(4)], out=zuT[:, 0:4, tc * 128:(tc + 1) * 128], in_=pt_[:].rearrange("p (a b) -> p a b", b=128), func=AF.Copy)
    n = 0
    for ts in range(NT):
        for dh in range(2):
            pb = n % 4; n += 1
            ps, rps = c.ps[pb], c.r_ps[pb]
            for kc in range(8):
                P.o("pe", "matmul", reads=[r_zu[kc][ts], r_wo], writes=[rps], out=ps[:], lhsT=zuT[:, kc, ts * 128:(ts + 1) * 128], rhs=wo[:, kc, dh * 512:(dh + 1) * 512], start=(kc == 0), stop=False)
            P.o("pe", "matmul", reads=[r_on, r_bo], writes=[rps], out=ps[:], lhsT=ones[0:1, :], rhs=bo[0:1, dh * 512:(dh + 1) * 512], start=False, stop=True)
            xr = c.xres[:, ts, dh * 512:(dh + 1) * 512]
            P.o("dve", "scalar_tensor_tensor", reads=[rps, c.r_xres[ts]], writes=[c.r_xres[ts]], out=xr, in0=xr, scalar=ALPHA, in1=ps[:], op0=ALU.mult, op1=ALU.add)
    c.touch_all([r for t in range(NT) for r in c.r_xT[t]])
```

```python
import math
from contextlib import ExitStack

import numpy as np
import ml_dtypes

import concourse.bass as bass
import concourse.mybir as mybir
from concourse.bass_utils import run_bass_kernel_spmd

F32 = mybir.dt.float32
BF16 = mybir.dt.bfloat16
AF = mybir.ActivationFunctionType
ALU = mybir.AluOpType

D = 1024
S = 2048
NT = S // 128
DEPTH = 4
ALPHA = (2 * DEPTH) ** 0.25
LN_EPS = 1e-5


class Res:
    __slots__ = ("name", "w", "r")
    ALL = []

    def __init__(self, name):
        self.name = name
        self.w = []
        self.r = []
        Res.ALL.append(self)


class Prog:
    ENG = ("pe", "act", "dve", "pool", "sp")
    NDMA = 12

    def __init__(self, nc, stack):
        self.nc = nc
        self.ops = {e: [] for e in self.ENG}
        self.cnt = {e: 0 for e in self.ENG}
        self.known = {e: {} for e in self.ENG}
        self.sem = {}
        for e in ("pe", "act", "dve", "pool"):
            self.sem[e] = stack.enter_context(nc.semaphore("s_" + e))
        for k in ("arr", "go", "ack"):
            self.sem[k] = stack.enter_context(nc.semaphore("b_" + k))
        Res.ALL = []
        self.dma_n = {}
        for q in ("sp", "pool", "act"):
            self.dma_n[q] = 0
            for i in range(self.NDMA):
                self.sem[("dma", q, i)] = stack.enter_context(nc.semaphore("d_%s_%d" % (q, i)))

    def _waits(self, eng, reads, writes):
        toks = []
        for r in reads:
            toks.extend(r.w)
        for r in writes:
            toks.extend(r.w)
            toks.extend(r.r)
        out = []
        kn = self.known[eng]
        best = {}
        for (k, v) in toks:
            if eng == "pe" and k == "pe":
                continue
            if v > kn.get(k, 0) and v > best.get(k, 0):
                best[k] = v
        for k, v in best.items():
            kn[k] = v
            out.append((k, v))
        return out

    def op(self, eng, fn, reads=(), writes=()):
        waits = self._waits(eng, reads, writes)
        self.cnt[eng] += 1
        tok = (eng, self.cnt[eng])
        self.ops[eng].append((waits, fn, (eng, 1)))
        for r in reads:
            r.r.append(tok)
        for r in writes:
            r.w = [tok]
            r.r = []
        return tok

    def o(self, eng, meth, reads=(), writes=(), **kw):
        return self.op(eng, lambda e, m=meth, kw=kw: getattr(e, m)(**kw), reads=reads, writes=writes)

    def dma(self, q, out, in_, reads=(), writes=()):
        i = self.dma_n[q]
        self.dma_n[q] += 1
        slot = i % self.NDMA
        key = ("dma", q, slot)
        prev = 16 * (i // self.NDMA)
        waits = self._waits(q, reads, writes)
        kn = self.known[q]
        if prev > kn.get(key, 0):
            kn[key] = prev
            waits.append((key, prev))
        tok = (key, prev + 16)
        self.ops[q].append((waits, lambda e, lv=None, o=out, i_=in_: e.dma_start(out=(o(lv) if callable(o) else o), in_=(i_(lv) if callable(i_) else i_)), (key, 16), "dma"))
        for r in reads:
            r.r.append(tok)
        for r in writes:
            r.w = [tok]
            r.r = []
        return tok

    def wait_all(self, eng, toks):
        waits = []
        kn = self.known[eng]
        for (k, v) in toks:
            if v > kn.get(k, 0):
                kn[k] = v
                waits.append((k, v))
        self.ops[eng].append((waits, None, None))

    def barrier(self, res_list):
        toks = [(e, self.cnt[e]) for e in ("pe", "act", "dve", "pool") if self.cnt[e] > 0]
        for q in ("sp", "pool", "act"):
            n = self.dma_n[q]
            for slot in range(min(n, self.NDMA)):
                last = ((n - 1 - slot) // self.NDMA) * self.NDMA + slot
                toks.append((("dma", q, slot), 16 * (last // self.NDMA + 1)))
        for r in res_list:
            r.r = list(toks)

    def loop_begin(self, n):
        for e in self.ENG:
            self.ops[e].append(("LB", n))

    def loop_end(self):
        for e in self.ENG:
            self.ops[e].append(("LE",))

    def sync_reset(self):
        def dma_fin(q):
            n = self.dma_n[q]
            return [(("dma", q, s_), 16 * ((n - 1 - s_) // self.NDMA + 1)) for s_ in range(min(n, self.NDMA))]
        for e in ("pe", "act", "dve", "pool"):
            w = [(e, self.cnt[e])] if self.cnt[e] > 0 else []
            if e in ("act", "pool"):
                w += dma_fin(e)
            self.ops[e].append(("BAR", w))
        self.ops["sp"].append(("BARSP", dma_fin("sp")))
        for e in self.ENG:
            self.cnt[e] = 0
            self.known[e] = {}
        for q in self.dma_n:
            self.dma_n[q] = 0
        for r in Res.ALL:
            r.w = []
            r.r = []

    def emit(self):
        nc = self.nc
        names = {"pe": "tensor", "act": "scalar", "dve": "vector", "pool": "gpsimd", "sp": "sync"}
        sem = self.sem
        body_sems = [v for k, v in sem.items() if k not in ("go", "ack")]

        def run(e, ops, pos, lv):
            while pos < len(ops):
                op_ = ops[pos]
                if op_[0] == "LB":
                    with e.Fori(0, op_[1]) as i:
                        pos = run(e, ops, pos + 1, i)
                    continue
                if op_[0] == "LE":
                    return pos + 1
                if op_[0] == "BAR":
                    for (k, v) in op_[1]:
                        e.wait_ge(sem[k], v)
                    e.sem_inc(sem["arr"], 1)
                    e.wait_ge(sem["go"], 1)
                    e.sem_inc(sem["ack"], 1)
                    pos += 1
                    continue
                if op_[0] == "BARSP":
                    for (k, v) in op_[1]:
                        e.wait_ge(sem[k], v)
                    e.wait_ge(sem["arr"], 4)
                    for sh in body_sems:
                        e.sem_clear(sh)
                    e.sem_inc(sem["go"], 1)
                    e.wait_ge(sem["ack"], 4)
                    e.sem_clear(sem["go"])
                    e.sem_clear(sem["ack"])
                    pos += 1
                    continue
                waits, fn, inc = op_[0], op_[1], op_[2]
                for (k, v) in waits:
                    e.wait_ge(sem[k], v)
                if fn is not None:
                    ins = fn(e, lv) if len(op_) == 4 else fn(e)
                    ins.then_inc(sem[inc[0]], inc[1])
                pos += 1
            return pos

        with nc.Block() as block:
            for eng in self.ENG:
                ops = self.ops[eng]
                getattr(block, names[eng])(lambda e, ops=ops: run(e, ops, 0, None))


class Ctx:
    pass


def build_program(cfg):
    NSEQ = cfg.get("nseq", 4)
    layers = cfg.get("layers", list(range(DEPTH)))
    nc = bass.Bass("TRN2", target_bir_lowering=False)
    with ExitStack() as stack:
        P = Prog(nc, stack)
        c = Ctx()
        c.nc, c.P, c.cfg, c.stack = nc, P, cfg, stack

        def dram_in(name, shape, dt=F32):
            return nc.dram_tensor(name, list(shape), dt, kind="ExternalInput").ap()

        def sb(name, shape, dt=F32):
            return stack.enter_context(nc.sbuf_tensor(name, list(shape), dt))

        c.dram_in, c.sb = dram_in, sb
        x_d = dram_in("x", [NSEQ, S, D])
        y_d = nc.dram_tensor("y", [NSEQ, S, D], F32, kind="ExternalOutput").ap()
        ident_d = dram_in("ident", [128, 128])
        lnp_d = dram_in("lnp", [DEPTH, 4, 128, D])

        NE = cfg.get("ne", 32)
        c.NE = NE
        c.wr_d = dram_in("moe_wr", [DEPTH, 128, 8, NE])
        c.br_d = dram_in("moe_br", [DEPTH, 128, NE])
        c.w1_d = dram_in("moe_w1", [DEPTH * NE * 8, 128, 8 * 256])
        c.w2_d = dram_in("moe_w2", [DEPTH * NE, 128, 8 * 1024])
        c.b1_d = dram_in("moe_b1", [DEPTH, 128, NE * 16])
        c.b2_d = dram_in("moe_b2", [DEPTH, NE, D])

        c.attn_d = {
            "LB": dram_in("att_LB", [128, 8, 2, 128], BF16), "RB": dram_in("att_RB", [128, 2, 512], BF16),
            "DG": dram_in("att_DG", [128, 8, 128], BF16), "idb": dram_in("att_idb", [128, 128], BF16),
            "cbt": dram_in("att_cbt", [128, 8, 31]), "wh": dram_in("att_wh", [16, 128, 8 * 384]),
            "wo": dram_in("att_wo", [2, 128, 8 * D]), "lqk": dram_in("att_lqk", [2, 128, 4, 64]),
            "subg": dram_in("att_subg", [2, 128, 128])}
        c.hy_d = {
            "featsT": dram_in("hy_featsT", [33, S]), "decay": dram_in("hy_decay", [128, 16, 512]),
            "F": dram_in("hy_F", [16, 128, 16 * 2 * 128], BF16), "G": dram_in("hy_G", [16, 128, 16 * 2 * 128], BF16),
            "sgn": dram_in("hy_sgn", [128, 16]), "f1w": dram_in("hy_f1w", [2, 33, 64]), "f2w": dram_in("hy_f2w", [2, 64, 64]),
            "f3w": dram_in("hy_f3w", [2, 64, 2048]), "fvec": dram_in("hy_fvec", [2, 64, 8]), "skip": dram_in("hy_skip", [2, 128, 2, 512]),
            "KF": nc.dram_tensor("hy_KF", [2 * 2 * 16, 128, 2, 512], F32, kind="Internal").ap()}
        c.r_KF = [[[Res("KF") for fi in range(16)] for o in range(2)] for j in range(2)]
        c.ev_d = {
            "win": dram_in("ev_win", [40, 128, 8 * 128]), "vec": dram_in("ev_vec", [2, 128, EV_N]),
            "wo": dram_in("ev_wo", [2, 128, 8 * D]), "bo": dram_in("ev_bo", [2, 1, D]), "ones": dram_in("ones128", [128, 128]),
            "X12": nc.dram_tensor("ev_X12", [2, 4, 128, 16 * 128], BF16, kind="Internal").ap()}
        c.r_X12 = [[Res("X12") for cc in range(4)] for w_ in range(2)]
        c.xres = sb("xres", [128, NT, D])
        c.xT = sb("xT", [128, 8, S], BF16)
        c.ident = sb("ident_sb", [128, 128])
        c.G = sb("G", [128, NT, NE])
        c.stats = [sb("stats%d" % i, [128, 2, 6]) for i in range(2)]
        c.mv = [sb("mv%d" % i, [128, 8]) for i in range(2)]
        c.epsc = sb("epsc", [128, 1])
        c.ps = [stack.enter_context(nc.psum_tensor("ps%d" % i, [128, 512], F32)) for i in range(8)]
        OVB = 109 * 1024
        c.ov = sb("ov", [128, OVB // 4])
        c.ov_off = 0
        c.bar_toks = []

        def carve(shape, dt=F32):
            n = 1
            for d_ in shape[1:]:
                n *= d_
            nb = n * (4 if dt == F32 else 2)
            nb = (nb + 31) // 32 * 32
            assert c.ov_off + nb <= OVB, ("overlay overflow", c.ov_off, nb)
            v = c.ov[0:shape[0], c.ov_off // 4:(c.ov_off + nb) // 4]
            c.ov_off += nb
            if dt != F32:
                v = v.bitcast(dt)
            v = v[:, 0:n]
            if len(shape) == 3:
                v = v.rearrange("p (a b) -> p a b", b=shape[2])
            elif len(shape) == 4:
                v = v.rearrange("p (a b c) -> p a b c", b=shape[2], c=shape[3])
            return v

        def newres(name):
            r = Res(name)
            r.r = list(c.bar_toks)
            r.w = list(c.bar_toks)
            return r

        def phase(name):
            c.ov_off = 0
            toks = [(e, P.cnt[e]) for e in ("pe", "act", "dve", "pool") if P.cnt[e] > 0]
            for q in ("sp", "pool", "act"):
                n = P.dma_n[q]
                for slot in range(min(n, P.NDMA)):
                    last = ((n - 1 - slot) // P.NDMA) * P.NDMA + slot
                    toks.append((("dma", q, slot), 16 * (last // P.NDMA + 1)))
            c.bar_toks = toks

        def touch_all(rl):
            phase("touch")
            for r_ in rl:
                r_.r = list(c.bar_toks)
                r_.w = list(c.bar_toks)

        c.touch_all = touch_all
        c.carve, c.newres, c.phase = carve, newres, phase
        c.dbg_toks = []

        def dbg(name, ap, shape, dt, reads):
            if not cfg.get("debug"):
                return
            t_ = nc.dram_tensor("dbg_" + name, list(shape), dt, kind="ExternalOutput").ap()
            c.dbg_toks.append(P.dma("sp", t_, ap, reads=reads))

        c.dbg = dbg

        c.r_xres = [Res("xres%d" % t) for t in range(NT)]
        c.r_xT = [(Res("xTa%d" % t), Res("xTb%d" % t)) for t in range(NT)]
        c.r_ps = [Res("ps%d" % i) for i in range(8)]
        c.r_ident = Res("ident")
        c.r_stats = [Res("st0"), Res("st1")]
        c.r_mv = [Res("mv0"), Res("mv1")]
        c.r_eps = Res("eps")
        c.r_G = [Res("G%d" % t) for t in range(NT)]
        c.r_GT = [Res("GT%d" % t) for t in range(NT)]
        c.lnp_d = lnp_d

        P.dma("sp", c.ident[:], ident_d[:, :], writes=[c.r_ident])
        P.op("dve", lambda e: e.memset(c.epsc[:], LN_EPS), writes=[c.r_eps])

        out_toks = []
        if cfg.get("mixer", True):
            hyena_prologue(c)
        if cfg.get("debug_kf"):
            kf_o = nc.dram_tensor("dbg_KF", [64, 128, 2, 512], F32, kind="ExternalOutput").ap()
            for i_ in range(32):
                c.dbg_toks.append(P.dma("sp", kf_o[i_], c.hy_d["KF"][i_], reads=[c.r_KF[0][i_ // 16][i_ % 16]]))
        if cfg.get("only_prologue"):
            layers = []
        P.sync_reset()
        P.loop_begin(NSEQ)
        c.bar_toks = []
        for t in range(NT):
            P.dma("sp", c.xres[:, t, :], (lambda lv, t=t: x_d[lv, t * 128:(t + 1) * 128, :]), writes=[c.r_xres[t]])
        build_xT(c)
        for l in layers:
            if cfg.get("mixer", True):
                if l % 2 == 0:
                    even_mixer(c, l // 2)
                else:
                    attention(c, l // 2)
            else:
                for t in range(NT):
                    P.op("pool", lambda e, t=t: e.tensor_scalar(out=c.xres[:, t, :], in0=c.xres[:, t, :], scalar1=ALPHA, scalar2=None, op0=ALU.mult), reads=[c.r_xres[t]], writes=[c.r_xres[t]])
            layer_norm(c, l, 0, router=True)
            if cfg.get("do_moe", True):
                moe(c, l)
                layer_norm(c, l, 1)
        for t in range(NT):
            P.dma("sp", (lambda lv, t=t: y_d[lv, t * 128:(t + 1) * 128, :]), c.xres[:, t, :], reads=[c.r_xres[t]])
        P.sync_reset()
        P.loop_end()
        P.emit()
    return nc


def make_stager(c, width, n=2):
    P = c.P
    st = [c.carve([128, width]) for i in range(n)]
    rs = [c.newres("stg") for i in range(n)]
    k = [0]

    def ld(dst, src, wres, view=None, eng="pool"):
        i = k[0] % n; k[0] += 1
        free = 1
        for d_ in dst.shape[1:]:
            free *= d_
        sv = st[i][:, 0:free]
        if view is not None:
            sv = view(sv)
        P.dma("sp", sv, src, writes=[rs[i]])
        if eng == "act":
            P.o("act", "activation", reads=[rs[i]], writes=wres, out=dst, in_=sv, func=AF.Copy)
        else:
            P.o(eng, "tensor_copy", reads=[rs[i]], writes=wres, out=dst, in_=sv)

    return ld


def build_xT(c):
    P = c.P
    c.phase("xT0")
    n = 0
    for t in range(NT):
        for half in range(2):
            pb = n % 4; n += 1
            ps, rps = c.ps[pb], c.r_ps[pb]
            for j in range(4):
                kc = half * 4 + j
                P.o("pe", "transpose", reads=[c.r_xres[t], c.r_ident], writes=[rps], out=ps[:, j * 128:(j + 1) * 128], in_=c.xres[:, t, kc * 128:(kc + 1) * 128], identity=c.ident[:])
            dst = c.xT[:, half * 4:(half + 1) * 4, t * 128:(t + 1) * 128]
            src = ps[:].rearrange("p (a b) -> p a b", b=128)
            if half == 0:
                P.o("act", "activation", reads=[rps], writes=[c.r_xT[t][half]], out=dst, in_=src, func=AF.Copy)
            else:
                P.o("dve", "tensor_copy", reads=[rps], writes=[c.r_xT[t][half]], out=dst, in_=src)


def layer_norm(c, l, which, router=False):
    P = c.P
    NE = c.NE
    c.phase("ln")
    c.GT = c.carve([32, S])
    c.r_GT = [c.newres("GT") for t in range(NT)]
    c.lng, c.lnb = c.carve([128, D]), c.carve([128, D])
    c.r_lng, c.r_lnb = c.newres("lng"), c.newres("lnb")
    c.lntmp = [c.carve([128, D]) for i in range(2)]
    c.r_lntmp = [c.newres("lntmp0"), c.newres("lntmp1")]
    c.xTf = [c.carve([128, 4, 128]) for i in range(2)]
    c.r_xTf = [c.newres("xTf0"), c.newres("xTf1")]
    c.wr, c.r_wr = c.carve([128, 8, NE]), c.newres("wr")
    c.br, c.r_br = c.carve([128, NE]), c.newres("br")
    c.rt = [c.carve([128, 4 * NE + 16]) for i in range(2)]
    c.r_rt = [c.newres("rt0"), c.newres("rt1")]
    P.dma("sp", c.lng, c.lnp_d[l, 2 * which, :, :], writes=[c.r_lng])
    P.dma("sp", c.lnb, c.lnp_d[l, 2 * which + 1, :, :], writes=[c.r_lnb])
    if router:
        P.dma("sp", c.wr, c.wr_d[l], writes=[c.r_wr])
        P.dma("sp", c.br, c.br_d[l], writes=[c.r_br])
    for t in range(NT):
        b = t % 2
        xr = c.xres[:, t, :]
        rx = c.r_xres[t]
        st, mv, tmp = c.stats[b], c.mv[b], c.lntmp[b]
        rst, rmv, rtmp = c.r_stats[b], c.r_mv[b], c.r_lntmp[b]
        P.op("dve", lambda e, st=st, xr=xr: e.bn_stats(out=st[:, 0, :], in_=xr[:, 0:512]), reads=[rx], writes=[rst])
        P.op("dve", lambda e, st=st, xr=xr: e.bn_stats(out=st[:, 1, :], in_=xr[:, 512:1024]), reads=[rx], writes=[rst])
        P.op("dve", lambda e, st=st, mv=mv: e.bn_aggr(out=mv[:, 0:2], in_=st[:].rearrange("p a b -> p (a b)")), reads=[rst], writes=[rmv])
        P.op("act", lambda e, mv=mv: e.activation(out=mv[:, 2:3], in_=mv[:, 1:2], func=AF.Sqrt, bias=c.epsc[:], scale=1.0), reads=[rmv, c.r_eps], writes=[rmv])
        P.op("dve", lambda e, mv=mv: e.reciprocal(out=mv[:, 3:4], in_=mv[:, 2:3]), reads=[rmv], writes=[rmv])
        P.op("dve", lambda e, mv=mv: e.tensor_scalar(out=mv[:, 4:5], in0=mv[:, 0:1], scalar1=mv[:, 3:4], scalar2=-1.0, op0=ALU.mult, op1=ALU.mult), reads=[rmv], writes=[rmv])
        P.op("act", lambda e, mv=mv, tmp=tmp, xr=xr: e.activation(out=tmp[:], in_=xr, func=AF.Identity, bias=mv[:, 4:5], scale=mv[:, 3:4]), reads=[rx, rmv], writes=[rtmp])
        P.op("pool", lambda e, tmp=tmp: e.tensor_tensor(out=tmp[:], in0=tmp[:], in1=c.lng[:], op=ALU.mult), reads=[rtmp, c.r_lng], writes=[rtmp])
        P.op("dve", lambda e, tmp=tmp, xr=xr: e.tensor_tensor(out=xr, in0=tmp[:], in1=c.lnb[:], op=ALU.add), reads=[rtmp, c.r_lnb], writes=[rx])
        for half in range(2):
            pb = 2 * b + half
            ps, rps = c.ps[pb], c.r_ps[pb]
            xf, rxf = c.xTf[half], c.r_xTf[half]
            for j in range(4):
                kc = half * 4 + j
                P.op("pe", lambda e, ps=ps, xr=xr, kc=kc, j=j: e.transpose(out=ps[:, j * 128:(j + 1) * 128], in_=xr[:, kc * 128:(kc + 1) * 128], identity=c.ident[:]), reads=[rx, c.r_ident], writes=[rps])
            dst = c.xT[:, half * 4:(half + 1) * 4, t * 128:(t + 1) * 128]
            if half == 0:
                P.op("act", lambda e, ps=ps, xf=xf: e.activation(out=xf[:], in_=ps[:].rearrange("p (a b) -> p a b", b=128), func=AF.Copy), reads=[rps], writes=[rxf])
            else:
                P.op("dve", lambda e, ps=ps, xf=xf: e.tensor_copy(out=xf[:], in_=ps[:].rearrange("p (a b) -> p a b", b=128)), reads=[rps], writes=[rxf])
            P.op("pool", lambda e, xf=xf, dst=dst: e.tensor_copy(out=dst, in_=xf[:]), reads=[rxf], writes=[c.r_xT[t][half]])
        if router:
            NE = c.NE
            psl, rpsl = c.ps[4 + b], c.r_ps[4 + b]
            for kc in range(8):
                xf = c.xTf[kc // 4]
                P.op("pe", lambda e, psl=psl, xf=xf, kc=kc: e.matmul(psl[:, 0:NE], lhsT=xf[:, kc % 4, :], rhs=c.wr[:, kc, :], start=(kc == 0), stop=(kc == 7)), reads=[c.r_xTf[kc // 4], c.r_wr], writes=[rpsl])
            rt, rrt = c.rt[b], c.r_rt[b]
            lg, ex, mk, t8 = rt[:, 0:NE], rt[:, NE:2 * NE], rt[:, 2 * NE:3 * NE], rt[:, 4 * NE:4 * NE + 8]
            nm, ss = rt[:, 4 * NE + 8:4 * NE + 9], rt[:, 4 * NE + 9:4 * NE + 10]
            Gt = c.G[:, t, :]
            P.op("dve", lambda e, lg=lg, psl=psl: e.tensor_tensor(out=lg, in0=psl[:, 0:NE], in1=c.br[:], op=ALU.add), reads=[rpsl, c.r_br], writes=[rrt])
            P.op("dve", lambda e, lg=lg, t8=t8: e.max(out=t8, in_=lg), reads=[rrt], writes=[rrt])
            P.op("dve", lambda e, lg=lg, t8=t8, mk=mk: e.tensor_scalar(out=mk, in0=lg, scalar1=t8[:, 3:4], scalar2=None, op0=ALU.is_ge), reads=[rrt], writes=[rrt])
            P.op("dve", lambda e, nm=nm, t8=t8: e.tensor_scalar(out=nm, in0=t8[:, 0:1], scalar1=-1.0, scalar2=None, op0=ALU.mult), reads=[rrt], writes=[rrt])
            P.op("act", lambda e, ex=ex, lg=lg, nm=nm: e.activation(out=ex, in_=lg, func=AF.Exp, bias=nm, scale=1.0), reads=[rrt], writes=[rrt])
            P.op("dve", lambda e, ex=ex, mk=mk: e.tensor_tensor(out=ex, in0=ex, in1=mk, op=ALU.mult), reads=[rrt], writes=[rrt])
            P.op("dve", lambda e, ex=ex, ss=ss: e.reduce_sum(out=ss, in_=ex, axis=mybir.AxisListType.X), reads=[rrt], writes=[rrt])
            P.op("dve", lambda e, ss=ss: e.reciprocal(out=ss, in_=ss), reads=[rrt], writes=[rrt])
            P.op("dve", lambda e, ex=ex, ss=ss, Gt=Gt: e.tensor_scalar(out=Gt, in0=ex, scalar1=ss, scalar2=None, op0=ALU.mult), reads=[rrt], writes=[c.r_G[t]])
            psg, rpsg = c.ps[6 + b], c.r_ps[6 + b]
            P.op("pe", lambda e, psg=psg, Gt=Gt: e.transpose(out=psg[0:NE, 0:128], in_=Gt, identity=c.ident[:]), reads=[c.r_G[t], c.r_ident], writes=[rpsg])
            P.op("act", lambda e, psg=psg, t=t: e.activation(out=c.GT[0:NE, t * 128:(t + 1) * 128], in_=psg[0:NE, 0:128], func=AF.Copy), reads=[rpsg], writes=[c.r_GT[t]])


def moe(c, l):
    P, NE = c.P, c.NE
    c.phase("moe")
    c.GT = c.carve([32, S])
    c.r_GT = [c.newres("GT") for t in range(NT)]
    c.actT = c.carve([128, 8, S], BF16); c.r_actT = [[c.newres("actT%d_%d" % (j, t)) for t in range(4)] for j in range(8)]
    c.w2 = [c.carve([128, 8, 1024], BF16) for i in range(1)]; c.r_w2 = [[c.newres("w2_%d_%d" % (i, h)) for h in range(8)] for i in range(1)]
    ld_w1 = make_stager(c, 2048, 2)
    ld_w2 = make_stager(c, 1024, 2)
    c.W1R = 2
    c.w1 = [c.carve([128, 8, 256], BF16) for i in range(c.W1R)]; c.r_w1 = [c.newres("w1_%d" % i) for i in range(c.W1R)]
    c.tA = [c.carve([128, 512]) for i in range(2)]; c.r_tA = [c.newres("tA0"), c.newres("tA1")]
    c.tB = [c.carve([128, 512]) for i in range(2)]; c.r_tB = [c.newres("tB0"), c.newres("tB1")]
    c.tC = [c.carve([128, 512]) for i in range(2)]; c.r_tC = [c.newres("tC0"), c.newres("tC1")]
    c.b1, c.r_b1 = c.carve([128, NE * 16]), c.newres("b1")
    c.b2, c.r_b2 = c.carve([32, D]), c.newres("b2")
    c.w1n = 0
    P.dma("sp", c.b1[:], c.b1_d[l], writes=[c.r_b1])
    P.dma("sp", c.b2[0:NE, :], c.b2_d[l], writes=[c.r_b2])
    n = 0
    for ts in range(NT):
        for dh in range(2):
            pb = 4 + (n % 4); n += 1
            ps, rps = c.ps[pb], c.r_ps[pb]
            P.op("pe", lambda e, ps=ps, ts=ts, dh=dh: e.matmul(ps[:], lhsT=c.GT[0:NE, ts * 128:(ts + 1) * 128], rhs=c.b2[0:NE, dh * 512:(dh + 1) * 512], start=True, stop=True), reads=[c.r_GT[ts], c.r_b2], writes=[rps])
            xr = c.xres[:, ts, dh * 512:(dh + 1) * 512]
            P.op("dve", lambda e, ps=ps, xr=xr: e.scalar_tensor_tensor(out=xr, in0=xr, scalar=ALPHA, in1=ps[:], op0=ALU.mult, op1=ALU.add), reads=[rps, c.r_xres[ts]], writes=[c.r_xres[ts]])
    hn = 0
    for ex in range(NE):
        w2, rw2 = c.w2[0], c.r_w2[0]
        for j in range(8):
            wi = c.w1n % c.W1R; c.w1n += 1
            w1, rw1 = c.w1[wi], c.r_w1[wi]
            ld_w1(w1, c.w1_d[(l * NE + ex) * 8 + j].rearrange("p (a b) -> p a b", b=256), [rw1], view=lambda v: v.rearrange("p (a b) -> p a b", b=256), eng="pool")
            ld_w2(w2[:, j, :], c.w2_d[l * NE + ex][:, j * 1024:(j + 1) * 1024], [rw2[j]], eng="act")
            for tt in range(4):
                sl = hn % 2; hn += 1
                pg, pu = c.ps[2 * sl], c.ps[2 * sl + 1]
                rpg, rpu = c.r_ps[2 * sl], c.r_ps[2 * sl + 1]
                xrd = [r for t in range(tt * 4, tt * 4 + 4) for r in c.r_xT[t]]
                for kc in range(8):
                    P.op("pe", lambda e, pg=pg, w1=w1, kc=kc, tt=tt: e.matmul(pg[:], lhsT=w1[:, kc, 0:128], rhs=c.xT[:, kc, tt * 512:(tt + 1) * 512], start=(kc == 0), stop=(kc == 7)), reads=[rw1] + xrd, writes=[rpg])
                for kc in range(8):
                    P.op("pe", lambda e, pu=pu, w1=w1, kc=kc, tt=tt: e.matmul(pu[:], lhsT=w1[:, kc, 128:256], rhs=c.xT[:, kc, tt * 512:(tt + 1) * 512], start=(kc == 0), stop=(kc == 7)), reads=[rw1] + xrd, writes=[rpu])
                tA, tB, tC = c.tA[sl], c.tB[sl], c.tC[sl]
                rA, rB, rC = c.r_tA[sl], c.r_tB[sl], c.r_tC[sl]
                bg = c.b1[:, ex * 16 + j:ex * 16 + j + 1]
                bu = c.b1[:, ex * 16 + 8 + j:ex * 16 + 8 + j + 1]
                P.op("dve", lambda e, tA=tA, pg=pg, bg=bg: e.tensor_scalar(out=tA[:], in0=pg[:], scalar1=bg, scalar2=7.0, op0=ALU.add, op1=ALU.min), reads=[rpg, c.r_b1], writes=[rA])
                P.op("act", lambda e, tA=tA, tB=tB: e.activation(out=tB[:], in_=tA[:], func=AF.Sigmoid, scale=1.702), reads=[rA], writes=[rB])
                P.op("act", lambda e, tC=tC, pu=pu, bu=bu: e.activation(out=tC[:], in_=pu[:], func=AF.Identity, bias=bu, scale=1.0), reads=[rpu, c.r_b1], writes=[rC])
                P.op("pool", lambda e, tA=tA, tB=tB: e.tensor_tensor(out=tA[:], in0=tA[:], in1=tB[:], op=ALU.mult), reads=[rA, rB], writes=[rA])
                P.op("pool", lambda e, tC=tC: e.tensor_scalar(out=tC[:], in0=tC[:], scalar1=-7.0, scalar2=7.0, op0=ALU.max, op1=ALU.min), reads=[rC], writes=[rC])
                dst = c.actT[:, j, tt * 512:(tt + 1) * 512]
                P.op("dve", lambda e, tA=tA, tC=tC, dst=dst: e.scalar_tensor_tensor(out=dst, in0=tC[:], scalar=1.0, in1=tA[:], op0=ALU.add, op1=ALU.mult), reads=[rA, rC], writes=[c.r_actT[j][tt]])
        for ts in range(NT):
            for dh in range(2):
                pb = 4 + (n % 4); n += 1
                ps, rps = c.ps[pb], c.r_ps[pb]
                for j in range(8):
                    P.op("pe", lambda e, ps=ps, j=j, ts=ts, dh=dh, w2=w2: e.matmul(ps[:], lhsT=c.actT[:, j, ts * 128:(ts + 1) * 128], rhs=w2[:, j, dh * 512:(dh + 1) * 512], start=(j == 0), stop=(j == 7)), reads=[c.r_actT[j][ts // 4], rw2[j]], writes=[rps])
                xr = c.xres[:, ts, dh * 512:(dh + 1) * 512]
                g = c.G[:, ts, ex:ex + 1]
                P.op("dve", lambda e, ps=ps, xr=xr, g=g: e.scalar_tensor_tensor(out=xr, in0=ps[:], scalar=g, in1=xr, op0=ALU.mult, op1=ALU.add), reads=[rps, c.r_xres[ts], c.r_G[ts]], writes=[c.r_xres[ts]])


_CACHE = {}


def prep_common(inp, ne=32):
    m = {}
    m["ident"] = np.eye(128, dtype=np.float32)
    lnp = np.stack([inp["ln1_g"], inp["ln1_b"], inp["ln2_g"], inp["ln2_b"]], axis=1)
    m["lnp"] = np.ascontiguousarray(np.broadcast_to(lnp[:, :, None, :], (DEPTH, 4, 128, D))).astype(np.float32)
    wr = inp["moe_w_r"][:, :, :ne]
    m["moe_wr"] = np.ascontiguousarray(wr.reshape(DEPTH, 8, 128, ne).transpose(0, 2, 1, 3))
    m["moe_br"] = np.ascontiguousarray(np.broadcast_to(inp["moe_b_r"][:, None, :ne], (DEPTH, 128, ne)))
    w1 = inp["moe_w1"][:, :ne]
    w1r = w1.reshape(DEPTH, ne, 8, 128, 2, 8, 128)
    w1r = w1r.transpose(0, 1, 5, 3, 2, 4, 6)
    m["moe_w1"] = np.ascontiguousarray(w1r).reshape(DEPTH * ne * 8, 128, 8 * 256)
    w2 = inp["moe_w2"][:, :ne]
    w2r = w2.reshape(DEPTH, ne, 8, 128, D).transpose(0, 1, 3, 2, 4)
    m["moe_w2"] = np.ascontiguousarray(w2r).reshape(DEPTH * ne, 128, 8 * 1024)
    b1 = inp["moe_b1"][:, :ne]
    m["moe_b1"] = np.ascontiguousarray(b1.reshape(DEPTH, ne, 16, 128).transpose(0, 3, 1, 2)).reshape(DEPTH, 128, ne * 16)
    m["moe_b2"] = np.ascontiguousarray(inp["moe_b2"][:, :ne])
    return m


def prep_all(inp, ne=32):
    m = prep_common(inp, ne)
    m.update(prep_attn(inp))
    m.update(prep_hyena(inp))
    m.update(prep_even(inp))
    return m


def kernel(**inputs):
    inp = {k: np.asarray(v) for k, v in inputs.items()}
    n_cores = 8
    nseq = inp["x"].shape[0] // n_cores
    nc = build_program(dict(nseq=nseq))
    shared = prep_all(inp)
    x = np.ascontiguousarray(inp["x"], dtype=np.float32)
    in_maps = []
    for i in range(n_cores):
        m = dict(shared)
        m["x"] = x[i * nseq:(i + 1) * nseq]
        in_maps.append(m)
    res = run_bass_kernel_spmd(nc, in_maps, core_ids=list(range(n_cores)))
    return np.concatenate([np.asarray(r["y"]) for r in res.results], axis=0).astype(np.float32)


N_HEADS = 8


def attention(c, j):
    P, nc = c.P, c.nc
    lam_init = 0.8 - 0.6 * math.exp(-0.3 * (2 * j + 1))
    c.phase("attn")
    cv, nr = c.carve, c.newres
    oT = cv([128, 8, S], BF16); r_oT = [[nr("oT") for t in range(NT)] for h in range(8)]
    wo = cv([128, 8, D], BF16); r_wo = nr("wo")
    wh = [cv([128, 8, 384], BF16) for i in range(2)]; r_wh = [nr("wh0"), nr("wh1")]
    qTm = [cv([128, S], BF16) for i in range(2)]; r_q = [[nr("q") for t in range(4)] for i in range(2)]
    r_qz = nr("qz")
    kT = cv([128, S], BF16); r_k = [nr("k") for t in range(4)]
    va = cv([128, NT, 132], BF16); r_v = [nr("v") for t in range(NT)]
    PT = [cv([128, 512], BF16) for i in range(4)]; r_PT = [nr("PT") for i in range(4)]
    LB = cv([128, 8, 2, 128], BF16); RB = cv([128, 2, 512], BF16); DG = cv([128, 8, 128], BF16)
    idb = cv([128, 128], BF16); cbt = cv([128, 8, 31]); r_cst = nr("cst")
    lqk = cv([128, 4, 64]); gs = cv([128, 128]); sm = cv([128, 16]); r_sm = nr("sm")
    fo = [cv([128, 128]) for i in range(4)]; r_fo = [nr("fo") for i in range(4)]
    f2 = [cv([128, 128]) for i in range(4)]; r_f2 = [nr("f2") for i in range(4)]
    fs = [cv([128, 8]) for i in range(4)]; r_fs = [nr("fs") for i in range(4)]
    d = c.attn_d
    P.dma("sp", LB, d["LB"][:, :, :, :], writes=[r_cst])
    P.dma("sp", RB, d["RB"][:, :, :], writes=[r_cst])
    P.dma("sp", DG, d["DG"][:, :, :], writes=[r_cst])
    P.dma("sp", idb, d["idb"][:, :], writes=[r_cst])
    P.dma("sp", cbt, d["cbt"][:, :, :], writes=[r_cst])
    P.dma("sp", lqk, d["lqk"][j], writes=[r_sm])
    P.dma("sp", gs, d["subg"][j], writes=[r_sm])
    ld_a = make_stager(c, 1024, 2)
    for hh in range(8):
        ld_a(wo[:, hh, :], d["wo"][j][:, hh * D:(hh + 1) * D], [r_wo])
    P.o("dve", "tensor_tensor", reads=[r_sm], writes=[r_sm], out=lqk[:, 0, :], in0=lqk[:, 0, :], in1=lqk[:, 1, :], op=ALU.mult)
    P.o("dve", "tensor_tensor", reads=[r_sm], writes=[r_sm], out=lqk[:, 2, :], in0=lqk[:, 2, :], in1=lqk[:, 3, :], op=ALU.mult)
    P.o("dve", "reduce_sum", reads=[r_sm], writes=[r_sm], out=sm[:, 0:1], in_=lqk[:, 0, :], axis=mybir.AxisListType.X)
    P.o("dve", "reduce_sum", reads=[r_sm], writes=[r_sm], out=sm[:, 1:2], in_=lqk[:, 2, :], axis=mybir.AxisListType.X)
    P.o("act", "activation", reads=[r_sm], writes=[r_sm], out=sm[:, 2:4], in_=sm[:, 0:2], func=AF.Exp)
    P.o("dve", "tensor_tensor", reads=[r_sm], writes=[r_sm], out=sm[:, 4:5], in0=sm[:, 3:4], in1=sm[:, 2:3], op=ALU.subtract)
    P.o("dve", "tensor_scalar", reads=[r_sm], writes=[r_sm], out=sm[:, 5:6], in0=sm[:, 4:5], scalar1=-lam_init, scalar2=None, op0=ALU.add)
    P.o("dve", "memset", reads=[], writes=[r_sm], ap=sm[:, 6:7], constant=LN_EPS)
    P.o("dve", "tensor_scalar", reads=[r_sm], writes=[r_sm], out=gs, in0=gs, scalar1=1.0 - lam_init, scalar2=None, op0=ALU.mult)
    mlam = sm[:, 5:6]
    epsa = sm[:, 6:7]
    P.o("pool", "memset", writes=[r_qz], ap=qTm[0][64:128, :], constant=0.0)
    P.o("pool", "memset", writes=[r_qz], ap=qTm[1][0:64, :], constant=0.0)
    for t in range(NT):
        P.o("pool", "memset", writes=[r_v[t]], ap=va[:, t, 128:129], constant=1.0)
    pj = 0
    sn = 0
    pn = 0
    fn = 0
    for h in range(8):
        w, rw = wh[h % 2], r_wh[h % 2]
        for part in range(3):
            ld_a(w[:, :, part * 128:(part + 1) * 128], d["wh"][j * 8 + h].rearrange("p (a t m) -> p a t m", t=3, m=128)[:, :, part, :], [rw], view=lambda v: v.rearrange("p (a b) -> p a b", b=128))
        for tt in range(4):
            xrd = [r for t in range(tt * 4, tt * 4 + 4) for r in c.r_xT[t]]
            cols = slice(tt * 512, (tt + 1) * 512)
            for qk in range(2):
                pb = 6 + pj % 2; pj += 1
                ps, rps = c.ps[pb], c.r_ps[pb]
                for kc in range(8):
                    P.o("pe", "matmul", reads=[rw] + xrd, writes=[rps], out=ps[:], lhsT=w[:, kc, qk * 128:(qk + 1) * 128], rhs=c.xT[:, kc, cols], start=(kc == 0), stop=(kc == 7))
                if qk == 0:
                    P.o("act", "activation", reads=[rps, r_qz], writes=[r_q[0][tt]], out=qTm[0][0:64, cols], in_=ps[0:64, :], func=AF.Copy, scale=0.125)
                    P.o("act", "activation", reads=[rps, r_qz], writes=[r_q[1][tt]], out=qTm[1][64:128, cols], in_=ps[64:128, :], func=AF.Copy, scale=0.125)
                else:
                    P.o("dve", "tensor_copy", reads=[rps], writes=[r_k[tt]], out=kT[:, cols], in_=ps[:])
        for ts in range(NT):
            pb = 6 + pj % 2; pj += 1
            ps, rps = c.ps[pb], c.r_ps[pb]
            for kc in range(8):
                P.o("pe", "matmul", reads=[rw] + list(c.r_xT[ts]), writes=[rps], out=ps[:, 0:128], lhsT=c.xT[:, kc, ts * 128:(ts + 1) * 128], rhs=w[:, kc, 256:384], start=(kc == 0), stop=(kc == 7))
            P.o("act" if ts % 2 == 0 else "dve", "activation" if ts % 2 == 0 else "tensor_copy", reads=[rps], writes=[r_v[ts]], out=va[:, ts, 0:128], in_=ps[:, 0:128], **({"func": AF.Copy} if ts % 2 == 0 else {}))
        for qt in range(4):
            qcols = slice(qt * 512, (qt + 1) * 512)
            for ci in range(2):
                for kt in range(NT):
                    pb = sn % 2; sn += 1
                    ps, rps = c.ps[pb], c.r_ps[pb]
                    nb = min(max(kt - qt * 4, 0), 4)
                    hasd = (qt * 4 <= kt < qt * 4 + 4)
                    ranges = []
                    if nb > 0:
                        ranges.append(("b", 0, nb * 128))
                    if hasd:
                        ranges.append(("d", nb * 128, (nb + 1) * 128))
                    lo_a = (nb + 1) * 128 if hasd else nb * 128
                    if lo_a < 512:
                        ranges.append(("a", lo_a, 512))
                    P.o("pe", "matmul", reads=[r_k[kt // 4], r_q[ci][qt]], writes=[rps], out=ps[:], lhsT=kT[:, kt * 128:(kt + 1) * 128], rhs=qTm[ci][:, qcols], start=True, stop=False)
                    for ri, (kind, lo, hi) in enumerate(ranges):
                        last = (ri == len(ranges) - 1)
                        if kind == "d":
                            P.o("pe", "matmul", reads=[r_cst], writes=[rps], out=ps[:, lo:hi], lhsT=idb, rhs=DG[:, h, :], start=False, stop=last)
                        else:
                            sg = 0 if kind == "a" else 1
                            P.o("pe", "matmul", reads=[r_cst], writes=[rps], out=ps[:, lo:hi], lhsT=LB[:, h, sg, :], rhs=RB[:, sg, lo:hi], start=False, stop=last)
                    pt, rpt = PT[pn % 4], r_PT[pn % 4]; pn += 1
                    dl = qt * 4 - kt
                    for (kind, lo, hi) in ranges:
                        idx = 15 if kind == "d" else (dl + 15 if kind == "a" else -dl + 15)
                        P.o("act", "activation", reads=[rps, r_cst], writes=[rpt], out=pt[:, lo:hi], in_=ps[:, lo:hi], func=AF.Exp, bias=cbt[:, h, idx:idx + 1], scale=1.0)
                    for sub in range(4):
                        acc, racc = c.ps[2 + sub], c.r_ps[2 + sub]
                        P.o("pe", "matmul", reads=[rpt, r_v[kt]], writes=[racc], out=acc[:, 0:129], lhsT=pt[:, sub * 128:(sub + 1) * 128], rhs=va[:, kt, 0:129], start=(kt == 0), stop=(kt == NT - 1))
                for sub in range(4):
                    acc, racc = c.ps[2 + sub], c.r_ps[2 + sub]
                    P.o("dve", "reciprocal", reads=[racc], writes=[r_fs[sub]], out=fs[sub][:, ci:ci + 1], in_=acc[:, 128:129])
                    if ci == 0:
                        P.o("dve", "tensor_scalar", reads=[racc, r_fs[sub]], writes=[r_fo[sub]], out=fo[sub], in0=acc[:, 0:128], scalar1=fs[sub][:, 0:1], scalar2=None, op0=ALU.mult)
                    else:
                        P.o("dve", "tensor_scalar", reads=[racc, r_fs[sub], r_sm], writes=[r_f2[sub]], out=f2[sub], in0=acc[:, 0:128], scalar1=fs[sub][:, 1:2], scalar2=mlam, op0=ALU.mult, op1=ALU.mult)
            for sub in range(4):
                ts = qt * 4 + sub
                b = sub
                P.o("pool", "tensor_tensor", reads=[r_fo[b], r_f2[b]], writes=[r_fo[b]], out=fo[b], in0=fo[b], in1=f2[b], op=ALU.add)
                P.o("act", "activation", reads=[r_fo[b]], writes=[r_f2[b]], out=f2[b], in_=fo[b], func=AF.Square)
                P.o("dve", "reduce_sum", reads=[r_f2[b]], writes=[r_fs[b]], out=fs[b][:, 2:3], in_=f2[b], axis=mybir.AxisListType.X)
                P.o("act", "activation", reads=[r_fs[b], r_sm], writes=[r_fs[b]], out=fs[b][:, 3:4], in_=fs[b][:, 2:3], func=AF.Sqrt, bias=epsa, scale=1.0 / 128.0)
                P.o("dve", "reciprocal", reads=[r_fs[b]], writes=[r_fs[b]], out=fs[b][:, 4:5], in_=fs[b][:, 3:4])
                P.o("dve", "scalar_tensor_tensor", reads=[r_fo[b], r_fs[b], r_sm], writes=[r_fo[b]], out=fo[b], in0=fo[b], scalar=fs[b][:, 4:5], in1=gs, op0=ALU.mult, op1=ALU.mult)
                pb = 6 + pj % 2; pj += 1
                ps, rps = c.ps[pb], c.r_ps[pb]
                P.o("pe", "transpose", reads=[r_fo[b], c.r_ident], writes=[rps], out=ps[:, 0:128], in_=fo[b], identity=c.ident[:])
                P.o("act", "activation", reads=[rps], writes=[r_oT[h][ts]], out=oT[:, h, ts * 128:(ts + 1) * 128], in_=ps[:, 0:128], func=AF.Copy)
    c.dbg("sm", sm, [128, 16], F32, [r_sm])
    c.dbg("kT", kT, [128, S], BF16, r_k)
    c.dbg("q0", qTm[0], [128, S], BF16, r_q[0] + [r_qz])
    c.dbg("va", va, [128, NT, 132], BF16, r_v)
    c.dbg("oT", oT, [128, 8, S], BF16, [r for hh in r_oT for r in hh])
    c.dbg("fo", fo[3], [128, 128], F32, [r_fo[3]])
    c.dbg("fs", fs[3], [128, 8], F32, [r_fs[3]])
    c.dbg("pt", PT[(pn - 1) % 4], [128, 512], BF16, [r_PT[(pn - 1) % 4]])
    n = 0
    for ts in range(NT):
        for dh in range(2):
            pb = 6 + n % 2; n += 1
            ps, rps = c.ps[pb], c.r_ps[pb]
            for h in range(8):
                P.o("pe", "matmul", reads=[r_oT[h][ts], r_wo], writes=[rps], out=ps[:], lhsT=oT[:, h, ts * 128:(ts + 1) * 128], rhs=wo[:, h, dh * 512:(dh + 1) * 512], start=(h == 0), stop=(h == 7))
            xr = c.xres[:, ts, dh * 512:(dh + 1) * 512]
            P.o("dve", "scalar_tensor_tensor", reads=[rps, c.r_xres[ts]], writes=[c.r_xres[ts]], out=xr, in0=xr, scalar=ALPHA, in1=ps[:], op0=ALU.mult, op1=ALU.add)


def attn_consts():
    sl = np.array([2.0 ** (-(h + 1)) for h in range(8)], dtype=np.float64)
    LB = np.zeros((128, 8, 2, 128), np.float64)
    m = np.arange(128)
    for h in range(8):
        LB[0, h, 0, :] = sl[h] * m
        LB[0, h, 1, :] = -sl[h] * m
        LB[1, h, :, :] = sl[h]
        LB[2, h, :, :] = sl[h]
    RB = np.zeros((128, 2, 512), np.float64)
    n = np.arange(512)
    RB[0, :, :] = 1.0
    RB[1, 0, :] = -256.0 * (n // 256); RB[1, 1, :] = 256.0 * (n // 256)
    RB[2, 0, :] = -(n % 256); RB[2, 1, :] = (n % 256)
    DG = np.zeros((128, 8, 128), np.float64)
    for h in range(8):
        DG[:, h, :] = -sl[h] * np.abs(m[None, :] - m[:, None])
    cbt = np.zeros((128, 8, 31), np.float64)
    for h in range(8):
        cbt[:, h, :] = -sl[h] * 128.0 * (np.arange(31) - 15)
    bf = ml_dtypes.bfloat16
    return {"att_LB": LB.astype(bf), "att_RB": RB.astype(bf), "att_DG": DG.astype(bf),
            "att_idb": np.eye(128).astype(bf), "att_cbt": cbt.astype(np.float32)}


def prep_attn(inp):
    m = attn_consts()
    wqkv = inp["attn_w_qkv"]
    n = wqkv.shape[0]
    q = wqkv[:, :, 0:1024].reshape(n, 8, 128, 8, 128)
    k = wqkv[:, :, 1024:2048].reshape(n, 8, 128, 8, 128)
    v = wqkv[:, :, 2048:3072].reshape(n, 8, 128, 8, 128)
    wh = np.stack([q, k, v], axis=4)
    wh = wh.transpose(0, 3, 2, 1, 4, 5)
    m["att_wh"] = np.ascontiguousarray(wh).reshape(n * 8, 128, 8 * 384)
    wo = inp["attn_w_out"].reshape(n, 8, 128, D).transpose(0, 2, 1, 3)
    m["att_wo"] = np.ascontiguousarray(wo).reshape(n, 128, 8 * D)
    lqk = np.stack([inp["attn_lq1"], inp["attn_lk1"], inp["attn_lq2"], inp["attn_lk2"]], axis=1)
    m["att_lqk"] = np.ascontiguousarray(np.broadcast_to(lqk[:, None], (n, 128, 4, 64)))
    m["att_subg"] = np.ascontiguousarray(np.broadcast_to(inp["attn_subln_g"][:, None, :], (n, 128, 128)))
    return m


TWO_PI = 2.0 * math.pi
MAGIC = 12582912.0


def hyena_consts():
    L, N = S, 2 * S
    f64 = np.float64
    pos = np.arange(L, dtype=np.float32)
    t = np.linspace(0.0, 1.0, L, dtype=np.float32)[:, None]
    bands = 16
    f = np.linspace(1e-4, bands - 1, bands, dtype=np.float32)
    ang = (np.float32(2.0 * math.pi / L) * pos[:, None] * f[None, :]).astype(np.float32)
    feats = np.concatenate([t, np.cos(ang), -np.sin(ang)], axis=-1).astype(np.float32)
    max_decay = math.log(1e-2) / 0.3
    min_decay = math.log(1e-2) / 1.5
    deltas = np.abs(np.linspace(min_decay, max_decay, 512, dtype=np.float32))
    decay = np.exp(-t * deltas[None, :]).astype(np.float32)
    s = np.arange(L, dtype=f64)
    angm = 2.0 * math.pi * np.outer(s, s) / N
    FcT = np.cos(angm)
    FsT = -np.sin(angm)
    FsT[:, 0] = (-1.0) ** s
    Gc = (2.0 / N) * np.cos(angm)
    Gc[0, :] = 1.0 / N
    Gs = -(2.0 / N) * np.sin(angm)
    Gs[0, :] = (1.0 / N) * (-1.0) ** s
    bf = ml_dtypes.bfloat16

    def blk(A, B):
        X = np.stack([A, B], axis=0).reshape(2, 16, 128, 16, 128)
        return np.ascontiguousarray(X.transpose(3, 2, 1, 0, 4)).astype(bf).reshape(16, 128, 16 * 2 * 128)

    sgn = -np.ones((128, 16), np.float32)
    sgn[0, 0] = 1.0
    return {"hy_featsT": np.ascontiguousarray(feats.T), "hy_decay": np.ascontiguousarray(decay.reshape(16, 128, 512).transpose(1, 0, 2)),
            "hy_F": blk(FcT, FsT), "hy_G": blk(Gc, Gs), "hy_sgn": sgn}


def hyena_prologue(c):
    P, nc = c.P, c.nc
    d = c.hy_d
    for j in range(2):
        if (2 * j) not in c.cfg.get("layers", list(range(DEPTH))):
            continue
        c.phase("hyfilt")
        cv, nr = c.carve, c.newres
        Tf = [cv([128, 16, 512], BF16) for g in range(4)]; r_Tf = [[nr("Tf") for t in range(16)] for g in range(4)]
        keepf = c.ov_off
        featsT = cv([33, S]); r_ft = nr("ft")
        f1w = cv([33, 64]); f2w = cv([64, 64]); f3w = cv([64, 2048]); fvec = cv([64, 8]); r_fw = nr("fw")
        xTf32 = c.xT[:].rearrange("p a b -> p (a b)").bitcast(F32)
        h1 = xTf32[0:64, 0:S]; r_h1 = nr("h1")
        h2 = xTf32[0:64, S:2 * S]; r_h2 = nr("h2")
        tmp = [cv([64, 512]) for i in range(2)]; r_tmp = [nr("t0"), nr("t1")]
        tk = [cv([64, 512]) for i in range(2)]; r_tk = [nr("k0"), nr("k1")]
        dec = [cv([128, 512]) for i in range(2)]; r_dec = [nr("d0"), nr("d1")]
        P.dma("sp", featsT, d["featsT"][:, :], writes=[r_ft])
        P.dma("sp", f1w, d["f1w"][j], writes=[r_fw])
        P.dma("sp", f2w, d["f2w"][j], writes=[r_fw])
        P.dma("sp", f3w, d["f3w"][j], writes=[r_fw])
        P.dma("sp", fvec, d["fvec"][j], writes=[r_fw])

        def sin_layer(src_fn, K, bcol, fcol, dst, rdst, rsrc):
            for nt in range(4):
                b = nt % 2
                ps, rps = c.ps[b], c.r_ps[b]
                src_fn(ps, rps, nt)
                tm, rtm, kk, rkk = tmp[b], r_tmp[b], tk[b], r_tk[b]
                P.o("dve", "tensor_scalar", reads=[rps, r_fw], writes=[rtm], out=tm, in0=ps[0:64, :], scalar1=fvec[:, bcol:bcol + 1], scalar2=fvec[:, fcol:fcol + 1], op0=ALU.add, op1=ALU.mult)
                P.o("dve", "tensor_scalar", reads=[rtm], writes=[rkk], out=kk, in0=tm, scalar1=1.0 / TWO_PI, scalar2=MAGIC, op0=ALU.mult, op1=ALU.add)
                P.o("dve", "tensor_scalar", reads=[rkk], writes=[rkk], out=kk, in0=kk, scalar1=-MAGIC, scalar2=None, op0=ALU.add)
                P.o("dve", "scalar_tensor_tensor", reads=[rkk, rtm], writes=[rtm], out=tm, in0=kk, scalar=-TWO_PI, in1=tm, op0=ALU.mult, op1=ALU.add)
                P.o("dve", "tensor_scalar", reads=[rtm], writes=[rtm], out=tm, in0=tm, scalar1=-3.141592, scalar2=3.141592, op0=ALU.max, op1=ALU.min)
                P.o("act", "activation", reads=[rtm], writes=[rdst], out=dst[:, nt * 512:(nt + 1) * 512], in_=tm, func=AF.Sin)

        def src1(ps, rps, nt):
            P.o("pe", "matmul", reads=[r_fw, r_ft], writes=[rps], out=ps[0:64, :], lhsT=f1w, rhs=featsT[:, nt * 512:(nt + 1) * 512], start=True, stop=True)

        def src2(ps, rps, nt):
            P.o("pe", "matmul", reads=[r_fw, r_h1], writes=[rps], out=ps[0:64, :], lhsT=f2w, rhs=h1[:, nt * 512:(nt + 1) * 512], start=True, stop=True)

        sin_layer(src1, 33, 0, 1, h1, r_h1, None)
        sin_layer(src2, 64, 2, 3, h2, r_h2, None)
        n = 0
        for nt in range(16):
            db, rdb = dec[nt % 2], r_dec[nt % 2]
            P.dma("sp", db, d["decay"][:, nt, :], writes=[rdb])
            for g in range(4):
                pb = 2 + n % 2; n += 1
                ps, rps = c.ps[pb], c.r_ps[pb]
                P.o("pe", "matmul", reads=[r_h2, r_fw], writes=[rps], out=ps[:], lhsT=h2[:, nt * 128:(nt + 1) * 128], rhs=f3w[:, g * 512:(g + 1) * 512], start=True, stop=True)
                P.o("dve", "tensor_tensor", reads=[rps, rdb], writes=[r_Tf[g][nt]], out=Tf[g][:, nt, :], in0=ps[:], in1=db, op=ALU.mult)
        for g in (2, 3):
            P.o("dve", "memset", writes=[r_Tf[g][0]], ap=Tf[g][0:1, 0, :], constant=0.0)
        c.phase("hyspec"); c.ov_off = keepf
        Fb = [cv([128, 16, 2, 128], BF16) for i in range(2)]; r_Fb = [nr("F0"), nr("F1")]
        skb = cv([128, 2, 512]); sgn = cv([128, 16]); r_sk = nr("sk")
        ko = [cv([128, 2, 512]) for i in range(2)]; r_ko = [nr("ko0"), nr("ko1")]
        pt = [cv([128, 512]) for i in range(2)]; r_pt = [nr("pt0"), nr("pt1")]
        P.dma("sp", skb, d["skip"][j], writes=[r_sk])
        P.dma("sp", sgn, d["sgn"][:, :], writes=[r_sk])
        n = 0
        for fi in range(16):
            fb, rfb = Fb[fi % 2], r_Fb[fi % 2]
            P.dma("sp", fb, d["F"][fi].rearrange("p (a b m) -> p a b m", b=2, m=128), writes=[rfb])
            for o in range(2):
                kb, rkb = ko[(fi * 2 + o) % 2], r_ko[(fi * 2 + o) % 2]
                for cs in range(2):
                    p1, rp1 = c.ps[4 + 2 * (n % 2)], c.r_ps[4 + 2 * (n % 2)]
                    p2, rp2 = c.ps[5 + 2 * (n % 2)], c.r_ps[5 + 2 * (n % 2)]
                    n += 1
                    for sc in range(16):
                        P.o("pe", "matmul", reads=[rfb, r_Tf[o][sc]], writes=[rp1], out=p1[:], lhsT=fb[:, sc, cs, :], rhs=Tf[o][:, sc, :], start=(sc == 0), stop=(sc == 15))
                    for sc in range(16):
                        P.o("pe", "matmul", reads=[rfb, r_Tf[2 + o][sc]], writes=[rp2], out=p2[:], lhsT=fb[:, sc, cs, :], rhs=Tf[2 + o][:, sc, :], start=(sc == 0), stop=(sc == 15))
                    tb, rtb = pt[cs], r_pt[cs]
                    P.o("act", "activation", reads=[rp1], writes=[rtb], out=tb, in_=p1[:], func=AF.Copy)
                    if cs == 0:
                        P.o("dve", "tensor_tensor", reads=[rp2, rtb], writes=[rtb], out=tb, in0=p2[:], in1=tb, op=ALU.add)
                        P.o("pool", "tensor_tensor", reads=[rtb, r_sk], writes=[rkb], out=kb[:, 0, :], in0=tb, in1=skb[:, o, :], op=ALU.add)
                    else:
                        P.o("dve", "scalar_tensor_tensor", reads=[rp2, rtb, r_sk], writes=[rkb], out=kb[:, 1, :], in0=p2[:], scalar=sgn[:, fi:fi + 1], in1=tb, op0=ALU.mult, op1=ALU.add)
                        if fi == 0:
                            P.o("dve", "tensor_tensor", reads=[rkb, r_sk], writes=[rkb], out=kb[0:1, 1, :], in0=kb[0:1, 1, :], in1=skb[0:1, o, :], op=ALU.add)
                P.dma("sp", d["KF"][(j * 2 + o) * 16 + fi], kb, reads=[rkb], writes=[c.r_KF[j][o][fi]])
        c.touch_all([r for t in range(NT) for r in c.r_xT[t]])
        if c.cfg.get("debug_kf"):
            c.dbg("h1_%d" % j, h1, [64, S], F32, [r_h1])
            c.dbg("h2_%d" % j, h2, [64, S], F32, [r_h2])
            c.dbg("Tf0_%d" % j, Tf[0], [128, 16, 512], BF16, r_Tf[0])


def prep_hyena(inp):
    m = hyena_consts()
    n = inp["hy_f1_w"].shape[0]
    m["hy_f1w"] = np.ascontiguousarray(inp["hy_f1_w"])
    m["hy_f2w"] = np.ascontiguousarray(inp["hy_f2_w"])
    m["hy_f3w"] = np.ascontiguousarray(inp["hy_f3_w"])
    fv = np.zeros((n, 64, 8), np.float32)
    fv[:, :, 0] = inp["hy_f1_b"]; fv[:, :, 1] = inp["hy_f1_freq"]; fv[:, :, 2] = inp["hy_f2_b"]; fv[:, :, 3] = inp["hy_f2_freq"]
    m["hy_fvec"] = fv
    m["hy_skip"] = np.ascontiguousarray(np.broadcast_to(inp["hy_skip"][:, None], (n, 128, 2, 512)))
    return m


EV_BIN, EV_SW, EV_SB, EV_DW, EV_DB, EV_LG, EV_LB, EV_N = 0, 20, 56, 68, 192, 196, 200, 204


def prep_even(inp):
    m = {}
    n = inp["hy_cf_w_in"].shape[0]
    w = inp["hy_cf_w_in"].reshape(n, 8, 128, 20, 128).transpose(0, 3, 2, 1, 4)
    m["ev_win"] = np.ascontiguousarray(w).reshape(n * 20, 128, 8 * 128)
    evv = np.zeros((n, 128, EV_N), np.float32)
    evv[:, :, EV_BIN:EV_BIN + 20] = inp["hy_cf_b_in"].reshape(n, 20, 128).transpose(0, 2, 1)
    evv[:, :, EV_SW:EV_SW + 36] = inp["hy_short_w"].reshape(n, 3, 12, 128).transpose(0, 3, 2, 1).reshape(n, 128, 36)
    evv[:, :, EV_SB:EV_SB + 12] = inp["hy_short_b"].reshape(n, 12, 128).transpose(0, 2, 1)
    evv[:, :, EV_DW:EV_DW + 124] = inp["cf_dw_w"].reshape(n, 31, 4, 128).transpose(0, 3, 2, 1).reshape(n, 128, 124)
    evv[:, :, EV_DB:EV_DB + 4] = inp["cf_dw_b"].reshape(n, 4, 128).transpose(0, 2, 1)
    evv[:, :, EV_LG:EV_LG + 4] = inp["cf_ln_g"].reshape(n, 4, 128).transpose(0, 2, 1)
    evv[:, :, EV_LB:EV_LB + 4] = inp["cf_ln_b"].reshape(n, 4, 128).transpose(0, 2, 1)
    m["ev_vec"] = evv
    wo = inp["even_w_out"].reshape(n, 8, 128, D).transpose(0, 2, 1, 3)
    m["ev_wo"] = np.ascontiguousarray(wo).reshape(n, 128, 8 * D)
    m["ev_bo"] = np.ascontiguousarray(inp["even_b_out"].reshape(n, 1, D))
    m["ones128"] = np.ones((128, 128), np.float32)
    return m


def even_mixer(c, j):
    P, nc = c.P, c.nc
    d = c.ev_d
    hd = c.hy_d
    X1, X2 = d["X12"][0], d["X12"][1]
    c.phase("even1")
    cv, nr = c.carve, c.newres
    zuT = cv([128, 8, S], BF16); r_zu = [[nr("zu") for t in range(NT)] for k in range(8)]
    evv = cv([128, EV_N]); r_ev = nr("evv")
    ones = cv([128, 128]); r_on = nr("ones")
    keep = c.ov_off
    P.dma("sp", evv, d["vec"][j], writes=[r_ev])
    P.dma("sp", ones, d["ones"][:, :], writes=[r_on])
    wch = [cv([128, 8, 128], BF16) for i in range(3)]; r_wch = [nr("wch") for i in range(3)]
    wn = [0]

    ldst = [make_stager(c, 1024, 2)]

    def load_w(ch):
        i = wn[0] % 3; wn[0] += 1
        ldst[0](wch[i], d["win"][j * 20 + ch].rearrange("p (a b) -> p a b", b=128), [r_wch[i]], view=lambda v: v.rearrange("p (a b) -> p a b", b=128))
        return wch[i], r_wch[i]

    pbn = [0]

    def proj(ch, evac):
        w, rw = load_w(ch)
        for tt in range(4):
            pb = pbn[0] % 2; pbn[0] += 1
            ps, rps = c.ps[pb], c.r_ps[pb]
            xrd = [r for t in range(tt * 4, tt * 4 + 4) for r in c.r_xT[t]]
            for kc in range(8):
                P.o("pe", "matmul", reads=[rw] + xrd, writes=[rps], out=ps[:], lhsT=w[:, kc, :], rhs=c.xT[:, kc, tt * 512:(tt + 1) * 512], start=(kc == 0), stop=(kc == 7))
            evac(ps, rps, tt)

    acc = [cv([128, S]) for i in range(4)]; r_acc = [[nr("acc") for t in range(4)] for i in range(4)]
    off_ub = c.ov_off
    ub = [cv([128, S]) for i in range(2)]; r_ub = [[nr("ub") for t in range(4)] for i in range(2)]
    sgb = [cv([128, 512]) for i in range(2)]; r_sgb = [nr("sg0"), nr("sg1")]
    sn = [0]
    for cc in range(4):
        u, ru = ub[cc % 2], r_ub[cc % 2]

        def evac_a(ps, rps, tt, u=u, ru=ru, cc=cc):
            P.o("act", "activation", reads=[rps, r_ev], writes=[ru[tt]], out=u[:, tt * 512:(tt + 1) * 512], in_=ps[:], func=AF.Identity, bias=evv[:, EV_BIN + 12 + cc:EV_BIN + 13 + cc], scale=1.0)

        def evac_g(ps, rps, tt, u=u, ru=ru, cc=cc):
            i = sn[0] % 2; sn[0] += 1
            P.o("act", "activation", reads=[rps, r_ev], writes=[r_sgb[i]], out=sgb[i], in_=ps[:], func=AF.Sigmoid, bias=evv[:, EV_BIN + 16 + cc:EV_BIN + 17 + cc], scale=1.0)
            P.o("pool", "tensor_tensor", reads=[r_sgb[i], ru[tt]], writes=[ru[tt]], out=u[:, tt * 512:(tt + 1) * 512], in0=u[:, tt * 512:(tt + 1) * 512], in1=sgb[i], op=ALU.mult)

        proj(12 + cc, evac_a)
        proj(16 + cc, evac_g)
        a, ra = acc[cc], r_acc[cc]
        wcol = lambda k, cc=cc: evv[:, EV_DW + cc * 31 + k:EV_DW + cc * 31 + k + 1]
        P.o("dve", "tensor_scalar", reads=list(ru) + [r_ev], writes=list(ra), out=a, in0=u, scalar1=wcol(15), scalar2=evv[:, EV_DB + cc:EV_DB + cc + 1], op0=ALU.mult, op1=ALU.add)
        for k in range(31):
            if k == 15:
                continue
            sft = k - 15
            lo, hi = max(0, -sft), min(S, S - sft)
            P.o("dve", "scalar_tensor_tensor", reads=list(ru) + [r_ev], writes=list(ra), out=a[:, lo:hi], in0=u[:, lo + sft:hi + sft], scalar=wcol(k), in1=a[:, lo:hi], op0=ALU.mult, op1=ALU.add)
    c.phase("even1b"); c.ov_off = off_ub
    sq = [cv([128, 512]) for i in range(2)]; r_sq = [nr("sq0"), nr("sq1")]
    mb = [cv([128, 512]) for i in range(2)]; r_mb = [nr("mb0"), nr("mb1")]
    rb = [cv([128, 512]) for i in range(2)]; r_rb = [nr("rb0"), nr("rb1")]
    zt = [cv([128, 512]) for i in range(2)]; r_zt = [nr("zt0"), nr("zt1")]
    eps_t = cv([128, 1]); r_e = nr("eps")
    P.o("dve", "memset", writes=[r_e], ap=eps_t, constant=LN_EPS)
    qn = 0
    for tt in range(4):
        cols = slice(tt * 512, (tt + 1) * 512)
        b = tt % 2
        p_s, rp_s = c.ps[2 + 2 * b], c.r_ps[2 + 2 * b]
        p_q, rp_q = c.ps[3 + 2 * b], c.r_ps[3 + 2 * b]
        for cc in range(4):
            P.o("pe", "matmul", reads=[r_on, r_acc[cc][tt]], writes=[rp_s], out=p_s[:], lhsT=ones, rhs=acc[cc][:, cols], start=(cc == 0), stop=(cc == 3))
        for cc in range(4):
            i = qn % 2; qn += 1
            P.o("act", "activation", reads=[r_acc[cc][tt]], writes=[r_sq[i]], out=sq[i], in_=acc[cc][:, cols], func=AF.Square)
            P.o("pe", "matmul", reads=[r_on, r_sq[i]], writes=[rp_q], out=p_q[:], lhsT=ones, rhs=sq[i], start=(cc == 0), stop=(cc == 3))
        m_, rm_, r_, rr_ = mb[b], r_mb[b], rb[b], r_rb[b]
        P.o("act", "activation", reads=[rp_s], writes=[rm_], out=m_, in_=p_s[:], func=AF.Copy, scale=1.0 / 512.0)
        P.o("dve", "tensor_tensor", reads=[rm_], writes=[rr_], out=r_, in0=m_, in1=m_, op=ALU.mult)
        P.o("dve", "scalar_tensor_tensor", reads=[rp_q, rr_], writes=[rr_], out=r_, in0=p_q[:], scalar=1.0 / 512.0, in1=r_, op0=ALU.mult, op1=ALU.subtract)
        P.o("act", "activation", reads=[rr_, r_e], writes=[rr_], out=r_, in_=r_, func=AF.Sqrt, bias=eps_t, scale=1.0)
        P.o("dve", "reciprocal", reads=[rr_], writes=[rr_], out=r_, in_=r_)
        for cc in range(4):
            i = qn % 2; qn += 1
            z, rz = zt[i], r_zt[i]
            a = acc[cc][:, cols]
            P.o("pool", "tensor_tensor", reads=[r_acc[cc][tt], rm_], writes=[r_acc[cc][tt]], out=a, in0=a, in1=m_, op=ALU.subtract)
            P.o("dve", "tensor_tensor", reads=[r_acc[cc][tt], rr_], writes=[r_acc[cc][tt]], out=a, in0=a, in1=r_, op=ALU.mult)
            P.o("act", "activation", reads=[r_acc[cc][tt], r_ev], writes=[r_acc[cc][tt]], out=a, in_=a, func=AF.Identity, bias=evv[:, EV_LB + cc:EV_LB + cc + 1], scale=evv[:, EV_LG + cc:EV_LG + cc + 1])
            P.o("act", "activation", reads=[r_acc[cc][tt]], writes=[rz], out=z, in_=a, func=AF.Sigmoid)
            P.o("pool", "tensor_tensor", reads=[r_acc[cc][tt], rz], writes=[r_zu[4 + cc][4 * tt + q] for q in range(4)], out=zuT[:, 4 + cc, cols], in0=a, in1=z, op=ALU.mult)
    c.phase("even2"); c.ov_off = keep
    V = cv([128, NT, 512], BF16); r_V = [nr("V") for t in range(NT)]
    keep2 = c.ov_off
    wch = [cv([128, 8, 128], BF16) for i in range(3)]; r_wch = [nr("wch") for i in range(3)]
    ldst[0] = make_stager(c, 1024, 2)
    pr = [cv([128, S]) for i in range(2)]; r_pr = [[nr("pr") for t in range(4)] for i in range(2)]
    hy = [cv([128, S]) for i in range(2)]; r_hy = [nr("hy0"), nr("hy1")]
    xs = [cv([128, NT, 128], BF16) for i in range(2)]; r_xs = [nr("xs0"), nr("xs1")]
    tn = 0
    for ch in range(12):
        p_, rp_ = pr[ch % 2], r_pr[ch % 2]
        h_, rh_ = hy[ch % 2], r_hy[ch % 2]

        def evac_h(ps, rps, tt, p_=p_, rp_=rp_, ch=ch):
            P.o("act", "activation", reads=[rps, r_ev], writes=[rp_[tt]], out=p_[:, tt * 512:(tt + 1) * 512], in_=ps[:], func=AF.Identity, bias=evv[:, EV_BIN + ch:EV_BIN + ch + 1], scale=1.0)

        proj(ch, evac_h)
        w0, w1_, w2_ = [evv[:, EV_SW + ch * 3 + k:EV_SW + ch * 3 + k + 1] for k in range(3)]
        P.o("pool", "tensor_scalar", reads=list(rp_) + [r_ev], writes=[rh_], out=h_, in0=p_, scalar1=w1_, scalar2=evv[:, EV_SB + ch:EV_SB + ch + 1], op0=ALU.mult, op1=ALU.add)
        P.o("dve", "scalar_tensor_tensor", reads=list(rp_) + [r_ev, rh_], writes=[rh_], out=h_[:, 1:S], in0=p_[:, 0:S - 1], scalar=w0, in1=h_[:, 1:S], op0=ALU.mult, op1=ALU.add)
        P.o("dve", "scalar_tensor_tensor", reads=list(rp_) + [r_ev, rh_], writes=[rh_], out=h_[:, 0:S - 1], in0=p_[:, 1:S], scalar=w2_, in1=h_[:, 0:S - 1], op0=ALU.mult, op1=ALU.add)
        which, cc = ch // 4, ch % 4
        if which < 2:
            x_, rx_ = xs[ch % 2], r_xs[ch % 2]
        for tg in range(4):
            pb = 4 + tn % 4; tn += 1
            ps, rps = c.ps[pb], c.r_ps[pb]
            for q in range(4):
                t = tg * 4 + q
                P.o("pe", "transpose", reads=[rh_, c.r_ident], writes=[rps], out=ps[:, q * 128:(q + 1) * 128], in_=h_[:, t * 128:(t + 1) * 128], identity=c.ident[:])
            src = ps[:].rearrange("p (a b) -> p a b", b=128)
            eng = "act" if tg % 2 == 0 else "dve"
            if which == 2:
                dst, wr_ = V[:, tg * 4:(tg + 1) * 4, cc * 128:(cc + 1) * 128], [r_V[tg * 4 + q] for q in range(4)]
            else:
                dst, wr_ = x_[:, tg * 4:(tg + 1) * 4, :], [rx_]
            if eng == "act":
                P.o("act", "activation", reads=[rps], writes=wr_, out=dst, in_=src, func=AF.Copy)
            else:
                P.o("dve", "tensor_copy", reads=[rps], writes=wr_, out=dst, in_=src)
        if which < 2:
            P.dma("sp", d["X12"][which][cc], x_[:].rearrange("p a b -> p (a b)"), reads=[rx_], writes=[c.r_X12[which][cc]])
    c.phase("even3"); c.ov_off = keep2
    Y = c.xT[:].rearrange("p a b -> p (a b)").rearrange("p (f s n) -> p f s n", s=2, n=512)
    r_Y = [nr("Y") for f in range(16)]
    wo = cv([128, 8, D], BF16); r_wo = nr("wo")
    bo = cv([1, D]); r_bo = nr("bo")
    ld_e = make_stager(c, 1024, 1)
    for kc in range(8):
        ld_e(wo[:, kc, :], d["wo"][j][:, kc * D:(kc + 1) * D], [r_wo])
    P.dma("sp", bo, d["bo"][j], writes=[r_bo])
    fg = [cv([128, 16, 2, 128], BF16) for i in range(2)]; r_fg = [nr("fg0"), nr("fg1")]
    kb = [cv([128, 2, 512]) for i in range(1)]; r_kb = [nr("kb0")]
    tq = [[cv([128, 512]) for k in range(4)] for i in range(1)]; r_tq = [[nr("tq") for k in range(4)] for i in range(1)]
    xg = [cv([128, 4, 128], BF16) for i in range(2)]; r_xg = [nr("xg0"), nr("xg1")]
    z2 = [cv([128, 512]) for i in range(1)]; r_z2 = [nr("z20")]
    fgn = 0
    for o in range(2):
        for fi in range(16):
            f_, rf_ = fg[fgn % 2], r_fg[fgn % 2]; fgn += 1
            P.dma("sp", f_, hd["F"][fi].rearrange("p (a b m) -> p a b m", b=2, m=128), writes=[rf_])
            k_, rk_ = kb[0], r_kb[0]
            P.dma("sp", k_, hd["KF"][(j * 2 + o) * 16 + fi], reads=[c.r_KF[j][o][fi]], writes=[rk_])
            b = fi % 2
            zc, rzc, zs, rzs = c.ps[2 * b], c.r_ps[2 * b], c.ps[2 * b + 1], c.r_ps[2 * b + 1]
            for cs, (pz, rpz) in enumerate(((zc, rzc), (zs, rzs))):
                for sc in range(16):
                    P.o("pe", "matmul", reads=[rf_, r_V[sc]], writes=[rpz], out=pz[:], lhsT=f_[:, sc, cs, :], rhs=V[:, sc, :], start=(sc == 0), stop=(sc == 15))
            t_, rt_ = tq[0], r_tq[0]
            P.o("dve", "tensor_tensor", reads=[rzc, rk_], writes=[rt_[0]], out=t_[0], in0=zc[:], in1=k_[:, 0, :], op=ALU.mult)
            P.o("dve", "tensor_tensor", reads=[rzs, rk_], writes=[rt_[1]], out=t_[1], in0=zs[:], in1=k_[:, 1, :], op=ALU.mult)
            P.o("dve", "tensor_tensor", reads=[rzc, rk_], writes=[rt_[2]], out=t_[2], in0=zc[:], in1=k_[:, 1, :], op=ALU.mult)
            P.o("dve", "tensor_tensor", reads=[rzs, rk_], writes=[rt_[3]], out=t_[3], in0=zs[:], in1=k_[:, 0, :], op=ALU.mult)
            P.o("pool", "tensor_tensor", reads=[rt_[0], rt_[1]], writes=[r_Y[fi]], out=Y[:, fi, 0, :], in0=t_[0], in1=t_[1], op=ALU.subtract)
            P.o("pool", "tensor_tensor", reads=[rt_[2], rt_[3]], writes=[r_Y[fi]], out=Y[:, fi, 1, :], in0=t_[2], in1=t_[3], op=ALU.add)
            if fi == 0:
                P.o("pool", "tensor_copy", reads=[rt_[0]], writes=[r_Y[fi]], out=Y[0:1, 0, 0, :], in_=t_[0][0:1, :])
                P.o("pool", "tensor_copy", reads=[rt_[1]], writes=[r_Y[fi]], out=Y[0:1, 0, 1, :], in_=t_[1][0:1, :])
        for tc in range(16):
            g_, rg_ = fg[fgn % 2], r_fg[fgn % 2]; fgn += 1
            P.dma("sp", g_, hd["G"][tc].rearrange("p (a b m) -> p a b m", b=2, m=128), writes=[rg_])
            x_, rx_ = xg[tc % 2], r_xg[tc % 2]
            P.dma("sp", x_, d["X12"][o][:, :, tc * 128:(tc + 1) * 128].rearrange("c p m -> p c m"), reads=list(c.r_X12[o]), writes=[rx_])
            pb = 4 + tc % 2
            ps, rps = c.ps[pb], c.r_ps[pb]
            for fi in range(16):
                for cs in range(2):
                    P.o("pe", "matmul", reads=[rg_, r_Y[fi]], writes=[rps], out=ps[:], lhsT=g_[:, fi, cs, :], rhs=Y[:, fi, cs, :], start=(fi == 0 and cs == 0), stop=(fi == 15 and cs == 1))
            xgv = x_[:].rearrange("p a b -> p (a b)")
            if o == 0:
                P.o("dve", "tensor_tensor", reads=[rps, rx_], writes=[r_V[tc]], out=V[:, tc, :], in0=ps[:], in1=xgv, op=ALU.mult)
            else:
                z_, rz_ = z2[0], r_z2[0]
                P.o("dve", "tensor_tensor", reads=[rps, rx_], writes=[rz_], out=z_, in0=ps[:], in1=xgv, op=ALU.mult)
                pt_, rpt_ = c.ps[6 + tc % 2], c.r_ps[6 + tc % 2]
                for cc in range(4):
                    P.o("pe", "transpose", reads=[rz_, c.r_ident], writes=[rpt_], out=pt_[:, cc * 128:(cc + 1) * 128], in_=z_[:, cc * 128:(cc + 1) * 128], identity=c.ident[:])
                P.o("act", "activation", reads=[rpt_], writes=[r_zu[cc][tc] for cc in range(4)], out=zuT[:, 0:4, tc * 128:(tc + 1) * 128], in_=pt_[:].rearrange("p (a b) -> p a b", b=128), func=AF.Copy)
    n = 0
    for ts in range(NT):
        for dh in range(2):
            pb = n % 4; n += 1
            ps, rps = c.ps[pb], c.r_ps[pb]
            for kc in range(8):
                P.o("pe", "matmul", reads=[r_zu[kc][ts], r_wo], writes=[rps], out=ps[:], lhsT=zuT[:, kc, ts * 128:(ts + 1) * 128], rhs=wo[:, kc, dh * 512:(dh + 1) * 512], start=(kc == 0), stop=False)
            P.o("pe", "matmul", reads=[r_on, r_bo], writes=[rps], out=ps[:], lhsT=ones[0:1, :], rhs=bo[0:1, dh * 512:(dh + 1) * 512], start=False, stop=True)
            xr = c.xres[:, ts, dh * 512:(dh + 1) * 512]
            P.o("dve", "scalar_tensor_tensor", reads=[rps, c.r_xres[ts]], writes=[c.r_xres[ts]], out=xr, in0=xr, scalar=ALPHA, in1=ps[:], op0=ALU.mult, op1=ALU.add)
    c.touch_all([r for t in range(NT) for r in c.r_xT[t]])
```
